# Optimizing a Trainium2 kernel written in Bass

```python
import jax, jax.numpy as jnp
from jax import lax
import numpy as np

D_MODEL = 1024
BATCH = 1
SEQ = 16384
DEPTH = 2

N_A_LAYERS = DEPTH // 2
N_B_LAYERS = DEPTH - N_A_LAYERS
HEAD_DIM = 64
SB_HEADS = D_MODEL // HEAD_DIM
NSA_HEADS = D_MODEL // HEAD_DIM
NSA_KV_GROUPS = 4
NSA_HPG = NSA_HEADS // NSA_KV_GROUPS
CMP_LEN = 32
CMP_STRIDE = 16
CMP_HIDDEN = 256
SEL_LEN = 64
SEL_TOPK = 16
WINDOW = 512
D_FF = 4 * D_MODEL
ROPE_THETA = 500000.0
ROT_DIM = HEAD_DIM // 4
Q_BLOCK = 128
NORM_EPS = 1e-5
NEG = -1e30
FORCED_SCORE = 1e6

kernel_name = "yoco_stickbreaking_nsa_hybrid"


def rms_norm(x, g):
    xf = x.astype(jnp.float32)
    y = xf * lax.rsqrt(jnp.mean(xf * xf, axis=-1, keepdims=True) + NORM_EPS)
    return (y * g.astype(jnp.float32)).astype(x.dtype)


def partial_rope(x):
    s = x.shape[1]
    half = ROT_DIM // 2
    inv_freq = ROPE_THETA ** (-jnp.arange(half, dtype=jnp.float32) * 2.0 / ROT_DIM)
    ang = jnp.arange(s, dtype=jnp.float32)[:, None] * inv_freq[None, :]
    cos = jnp.cos(ang)[None, :, None, :]
    sin = jnp.sin(ang)[None, :, None, :]
    xr = x[..., :ROT_DIM].astype(jnp.float32)
    x1, x2 = xr[..., :half], xr[..., half:]
    rot = jnp.concatenate([x1 * cos - x2 * sin, x2 * cos + x1 * sin], axis=-1).astype(x.dtype)
    return jnp.concatenate([rot, x[..., ROT_DIM:]], axis=-1)


def to_qblocks(a):
    b, s = a.shape[0], a.shape[1]
    return jnp.moveaxis(a.reshape((b, s // Q_BLOCK, Q_BLOCK) + a.shape[2:]), 1, 0)


def from_qblocks(a):
    a = jnp.moveaxis(a, 0, 1)
    return a.reshape((a.shape[0], a.shape[1] * a.shape[2]) + a.shape[3:])


def sqrelu_mlp(x, w1, w2):
    h = jax.nn.relu(x @ w1)
    return (h * h) @ w2


def stick_breaking_attention(h, w_qkv, w_o):
    b, s, _ = h.shape
    n_qb = s // Q_BLOCK
    qkv = (h @ w_qkv).reshape(b, s, 3, SB_HEADS, HEAD_DIM)
    q, k, v = qkv[:, :, 0], qkv[:, :, 1], qkv[:, :, 2]
    scale = HEAD_DIM ** -0.5
    key_pos = jnp.arange(s)

    def block(args):
        qi, q_blk = args
        t = qi * Q_BLOCK + jnp.arange(Q_BLOCK)
        z = jnp.einsum('bqhd,bshd->bhqs', q_blk, k).astype(jnp.float32) * scale
        causal = key_pos[None, :] < t[:, None]
        log_1m = jnp.where(causal, jax.nn.log_sigmoid(-z), 0.0)
        excl = lax.cumsum(log_1m, axis=3, reverse=True) - log_1m
        a = jnp.where(causal, jnp.exp(jax.nn.log_sigmoid(z) + excl), 0.0)
        return jnp.einsum('bhqs,bshd->bqhd', a.astype(v.dtype), v)

    o = lax.map(block, (jnp.arange(n_qb), to_qblocks(q)))
    return from_qblocks(o).reshape(b, s, SB_HEADS * HEAD_DIM) @ w_o


def nsa_shared_kv(h, kv_norm, w_kv, cmp_pos, cmp_w1, cmp_w2):
    b, s, _ = h.shape
    hn = rms_norm(h, kv_norm)
    kv = (hn @ w_kv).reshape(b, s, 3, 2, NSA_KV_GROUPS, HEAD_DIM)
    n_cmp = (s - CMP_LEN) // CMP_STRIDE + 1
    idx = jnp.arange(n_cmp)[:, None] * CMP_STRIDE + jnp.arange(CMP_LEN)[None, :]
    blk = kv[:, :, 0][:, idx]
    blk = blk + jnp.transpose(cmp_pos, (1, 0, 2))[None, None, :, :, None, :]
    blk = jnp.transpose(blk, (0, 1, 3, 4, 2, 5)).reshape(b, n_cmp, 2, NSA_KV_GROUPS, CMP_LEN * HEAD_DIM)
    hid = jax.nn.gelu(jnp.einsum('bncgf,cfe->bncge', blk, cmp_w1))
    cmp = jnp.einsum('bncge,ced->bncgd', hid, cmp_w2)
    k_cmp, v_cmp = cmp[:, :, 0], cmp[:, :, 1]
    k_slc, v_slc = partial_rope(kv[:, :, 1, 0]), kv[:, :, 1, 1]
    k_win, v_win = partial_rope(kv[:, :, 2, 0]), kv[:, :, 2, 1]
    return (k_cmp, v_cmp, k_slc, v_slc, k_win, v_win)


def nsa_attention(h, w_q, gate_b, w_o, shared):
    k_cmp, v_cmp, k_slc, v_slc, k_win, v_win = shared
    b, s, _ = h.shape
    G, HPG, DH = NSA_KV_GROUPS, NSA_HPG, HEAD_DIM
    n_cmp = k_cmp.shape[1]
    n_sel = s // SEL_LEN
    n_qb = s // Q_BLOCK
    topk = min(SEL_TOPK, n_sel)
    scale = DH ** -0.5

    proj = h @ w_q
    q = proj[..., :NSA_HEADS * DH].reshape(b, s, NSA_HEADS, DH)
    gates = jax.nn.sigmoid((proj[..., NSA_HEADS * DH:] + gate_b).astype(jnp.float32)).reshape(b, s, 3, NSA_HEADS)
    q_rot = partial_rope(q)

    cmp_start = jnp.arange(n_cmp) * CMP_STRIDE
    cmp_end = cmp_start + CMP_LEN - 1
    sel_ids = jnp.arange(n_sel)
    sel_start = sel_ids * SEL_LEN
    overlap = ((cmp_start[:, None] < sel_start[None, :] + SEL_LEN)
               & (cmp_start[:, None] + CMP_LEN > sel_start[None, :])).astype(jnp.float32)
    k_sel_blocks = jnp.transpose(k_slc.reshape(b, n_sel, SEL_LEN, G, DH), (0, 3, 1, 2, 4))
    v_sel_blocks = jnp.transpose(v_slc.reshape(b, n_sel, SEL_LEN, G, DH), (0, 3, 1, 2, 4))
    pad = ((0, 0), (WINDOW, 0), (0, 0), (0, 0))
    k_win_p = jnp.pad(k_win, pad)
    v_win_p = jnp.pad(v_win, pad)
    gather = jax.vmap(jax.vmap(lambda blocks, ids: blocks[ids]))

    def block(args):
        qi, q_blk, qr_blk = args
        q0 = qi * Q_BLOCK
        t = q0 + jnp.arange(Q_BLOCK)
        qg = q_blk.reshape(b, Q_BLOCK, G, HPG, DH)
        qrg = qr_blk.reshape(b, Q_BLOCK, G, HPG, DH)

        sc = jnp.einsum('bqghd,bngd->bghqn', qg, k_cmp).astype(jnp.float32) * scale
        valid_c = cmp_end[None, :] <= t[:, None]
        p_cmp = jax.nn.softmax(jnp.where(valid_c, sc, NEG), axis=-1)
        p_cmp = p_cmp * jnp.any(valid_c, axis=-1)[:, None]
        o_cmp = jnp.einsum('bghqn,bngd->bqghd', p_cmp.astype(v_cmp.dtype), v_cmp)

        imp = jnp.einsum('bghqn,nm->bgqm', p_cmp, overlap)
        blk_t = t // SEL_LEN
        forced = (sel_ids[None, :] == 0) | (sel_ids[None, :] == blk_t[:, None]) | (sel_ids[None, :] == blk_t[:, None] - 1)
        causal_b = sel_ids[None, :] <= blk_t[:, None]
        imp = jnp.where(causal_b, jnp.where(forced, FORCED_SCORE, imp), -jnp.inf)
        _, top_idx = lax.top_k(imp, topk)
        k_sel = gather(k_sel_blocks, top_idx)
        v_sel = gather(v_sel_blocks, top_idx)
        ss = jnp.einsum('bqghd,bgqkld->bghqkl', qrg, k_sel).astype(jnp.float32) * scale
        key_pos = top_idx[..., None] * SEL_LEN + jnp.arange(SEL_LEN)
        valid_s = key_pos <= t[None, None, :, None, None]
        ss = jnp.where(valid_s[:, :, None], ss, NEG)
        p_sel = jax.nn.softmax(ss.reshape(b, G, HPG, Q_BLOCK, topk * SEL_LEN), axis=-1).reshape(ss.shape)
        o_sel = jnp.einsum('bghqkl,bgqkld->bqghd', p_sel.astype(v_sel.dtype), v_sel)

        k_w = lax.dynamic_slice_in_dim(k_win_p, q0, WINDOW + Q_BLOCK, axis=1)
        v_w = lax.dynamic_slice_in_dim(v_win_p, q0, WINDOW + Q_BLOCK, axis=1)
        sw = jnp.einsum('bqghd,bsgd->bghqs', qrg, k_w).astype(jnp.float32) * scale
        kpos = q0 - WINDOW + jnp.arange(WINDOW + Q_BLOCK)
        diff = t[:, None] - kpos[None, :]
        valid_w = (diff >= 0) & (diff < WINDOW) & (kpos[None, :] >= 0)
        p_w = jax.nn.softmax(jnp.where(valid_w, sw, NEG), axis=-1)
        o_win = jnp.einsum('bghqs,bsgd->bqghd', p_w.astype(v_w.dtype), v_w)

        return jnp.stack([o_cmp, o_sel, o_win], axis=2).reshape(b, Q_BLOCK, 3, NSA_HEADS, DH)

    o = from_qblocks(lax.map(block, (jnp.arange(n_qb), to_qblocks(q), to_qblocks(q_rot))))
    out = jnp.einsum('bsch,bschd->bshd', gates.astype(o.dtype), o).reshape(b, s, NSA_HEADS * DH)
    return out @ w_o


def setup_inputs(seed: int = 0) -> dict:
    key = jax.random.key(seed)
    ks = jax.random.split(key, 16)
    f32 = jnp.float32

    def w(k, shape, fan_in):
        return jax.random.normal(k, shape, f32) * fan_in ** -0.5

    x = jax.random.normal(ks[0], (BATCH, SEQ, D_MODEL), f32)
    norm_gain = 1.0 + 0.02 * jax.random.normal(ks[1], (DEPTH, 2, D_MODEL), f32)
    sb_w_qkv = w(ks[2], (N_A_LAYERS, D_MODEL, 3 * SB_HEADS * HEAD_DIM), D_MODEL)
    sb_w_o = w(ks[3], (N_A_LAYERS, SB_HEADS * HEAD_DIM, D_MODEL), SB_HEADS * HEAD_DIM)
    kv_norm = 1.0 + 0.02 * jax.random.normal(ks[4], (D_MODEL,), f32)
    nsa_w_kv = w(ks[5], (D_MODEL, 3 * 2 * NSA_KV_GROUPS * HEAD_DIM), D_MODEL)
    cmp_pos = 0.1 * jax.random.normal(ks[6], (2, CMP_LEN, HEAD_DIM), f32)
    cmp_w1 = w(ks[7], (2, CMP_LEN * HEAD_DIM, CMP_HIDDEN), CMP_LEN * HEAD_DIM)
    cmp_w2 = w(ks[8], (2, CMP_HIDDEN, HEAD_DIM), CMP_HIDDEN)
    nsa_w_q = w(ks[9], (N_B_LAYERS, D_MODEL, NSA_HEADS * HEAD_DIM + 3 * NSA_HEADS), D_MODEL)
    nsa_gate_b = 0.1 * jax.random.normal(ks[10], (N_B_LAYERS, 3 * NSA_HEADS), f32)
    nsa_w_o = w(ks[11], (N_B_LAYERS, NSA_HEADS * HEAD_DIM, D_MODEL), NSA_HEADS * HEAD_DIM)
    mlp_w1 = w(ks[12], (DEPTH, D_MODEL, D_FF), D_MODEL)
    mlp_w2 = w(ks[13], (DEPTH, D_FF, D_MODEL), D_FF)
    final_norm = 1.0 + 0.02 * jax.random.normal(ks[14], (D_MODEL,), f32)
    return {"x": x, "norm_gain": norm_gain, "sb_w_qkv": sb_w_qkv, "sb_w_o": sb_w_o,
            "kv_norm": kv_norm, "nsa_w_kv": nsa_w_kv, "cmp_pos": cmp_pos, "cmp_w1": cmp_w1,
            "cmp_w2": cmp_w2, "nsa_w_q": nsa_w_q, "nsa_gate_b": nsa_gate_b, "nsa_w_o": nsa_w_o,
            "mlp_w1": mlp_w1, "mlp_w2": mlp_w2, "final_norm": final_norm}


def reference(x, norm_gain, sb_w_qkv, sb_w_o, kv_norm, nsa_w_kv, cmp_pos, cmp_w1, cmp_w2,
              nsa_w_q, nsa_gate_b, nsa_w_o, mlp_w1, mlp_w2, final_norm):
    h = x
    shared = None
    for layer in range(DEPTH):
        hn = rms_norm(h, norm_gain[layer, 0])
        if layer < N_A_LAYERS:
            h = h + stick_breaking_attention(hn, sb_w_qkv[layer], sb_w_o[layer])
        else:
            j = layer - N_A_LAYERS
            h = h + nsa_attention(hn, nsa_w_q[j], nsa_gate_b[j], nsa_w_o[j], shared)
        h = h + sqrelu_mlp(rms_norm(h, norm_gain[layer, 1]), mlp_w1[layer], mlp_w2[layer])
        if layer == N_A_LAYERS - 1:
            shared = nsa_shared_kv(h, kv_norm, nsa_w_kv, cmp_pos, cmp_w1, cmp_w2)
    return rms_norm(h, final_norm)
```

```python
import numpy as np
import ml_dtypes
import concourse.bass as bass
import concourse.mybir as mybir
from concourse.bass_utils import run_bass_kernel_spmd

F32 = mybir.dt.float32
BF16 = mybir.dt.bfloat16
AF = mybir.ActivationFunctionType
ALU = mybir.AluOpType
NPBF = ml_dtypes.bfloat16

S = 16384
D = 1024
NCORE = 8
EPS = 1e-5
MASKV = 30000.0


class Buf:
    __slots__ = ("w", "r", "dsem", "dcnt", "name")

    def __init__(self, name=""):
        self.w = None
        self.r = {}
        self.dsem = None
        self.dcnt = 0
        self.name = name


class K:
    ENGS = ("pe", "act", "dve", "pool", "sp")

    def __init__(self, nc):
        self.nc = nc
        self.ops = {e: [] for e in self.ENGS}
        self.esem = {e: nc.alloc_semaphore("es_" + e) for e in self.ENGS}
        self.nsem = 0

    def buf(self, name=""):
        return Buf(name)

    def bufs(self, n, name=""):
        return [Buf(name + str(i)) for i in range(n)]

    @staticmethod
    def _key(tok):
        return (tok[0], tok[1])

    def _deps(self, reads, writes):
        deps = {}

        def add(tok):
            k = self._key(tok)
            if k not in deps or deps[k][2] < tok[2]:
                deps[k] = tok

        for b in reads:
            if b.w is not None:
                add(b.w)
        for b in writes:
            if b.w is not None:
                add(b.w)
            for t in b.r.values():
                add(t)
        return list(deps.values())

    def _commit(self, tok, reads, writes):
        k = self._key(tok)
        for b in reads:
            b.r[k] = tok
        for b in writes:
            b.w = tok
            b.r = {}

    def op(self, eng, fn, reads=(), writes=()):
        deps = self._deps(reads, writes)
        idx = len(self.ops[eng])
        tok = ("c", eng, idx)
        self.ops[eng].append(["c", fn, deps, None])
        self._commit(tok, reads, writes)
        return tok

    def dma(self, eng, out, in_, reads=(), writes=(), sembuf=None):
        b = sembuf if sembuf is not None else (writes[0] if writes else reads[0])
        if b.dsem is None:
            b.dsem = self.nc.alloc_semaphore("ds%d" % self.nsem)
            self.nsem += 1
        deps = self._deps(reads, writes)
        b.dcnt += 16
        tok = ("d", b.dsem, b.dcnt)
        self.ops[eng].append(["d", (out, in_), deps, (b.dsem, 16)])
        self._commit(tok, reads, writes)
        return tok

    def wait(self, eng, bufs):
        deps = self._deps((), bufs)
        self.ops[eng].append(["w", None, deps, None])

    def emit(self):
        nc = self.nc
        needed = {e: set() for e in self.ENGS}
        for e in self.ENGS:
            for o in self.ops[e]:
                for t in o[2]:
                    if t[0] == "c":
                        if t[1] == "pe" and e == "pe":
                            continue
                        needed[t[1]].add(t[2])
        rank = {}
        for e in self.ENGS:
            for i, idx in enumerate(sorted(needed[e])):
                rank[(e, idx)] = i + 1
        engobj = {"pe": nc.tensor, "act": nc.scalar, "dve": nc.vector, "pool": nc.gpsimd, "sp": nc.sync}
        with nc.Block() as block:
            deco = {"pe": block.tensor, "act": block.scalar, "dve": block.vector,
                    "pool": block.gpsimd, "sp": block.sync}

            def make(e):
                def body(eng):
                    waited = {}
                    for idx, o in enumerate(self.ops[e]):
                        kind, fn, deps, inc = o
                        for t in deps:
                            if t[0] == "c":
                                if t[1] == "pe" and e == "pe":
                                    continue
                                sem = self.esem[t[1]]
                                val = rank[(t[1], t[2])]
                                key = ("c", t[1])
                            else:
                                sem = t[1]
                                val = t[2]
                                key = ("d", id(t[1]))
                            if waited.get(key, 0) < val:
                                eng.wait_ge(sem, val)
                                waited[key] = val
                        if kind == "c":
                            ins = fn(eng)
                            if (e, idx) in rank:
                                ins.then_inc(self.esem[e], 1)
                        elif kind == "d":
                            eng.dma_start(out=fn[0], in_=fn[1]).then_inc(inc[0], inc[1])
                return body

            for e in self.ENGS:
                if self.ops[e]:
                    deco[e](make(e))


def _bf(a):
    return np.ascontiguousarray(a.astype(NPBF))


def _consts_attn():
    c = {}
    j = np.arange(128)[:, None]
    s = np.arange(128)[None, :]
    c["uneg"] = _bf(np.where(j >= s, -1.0, 0.0))
    c["oneg"] = _bf(-np.ones((128, 128)))
    c["ident"] = _bf(np.eye(128))
    sl = np.arange(128)[:, None]
    tl = np.arange(512)[None, :]
    cm01 = np.stack([(sl + 128 * jj < tl) for jj in range(4)]).astype(np.float32)
    c["cm01"] = _bf(cm01.transpose(1, 0, 2))
    c["cmneg"] = _bf(((cm01 - 1.0) * MASKV).transpose(1, 0, 2))
    return c


def build_L1(ntok=S):
    nc = bass.Bass("TRN2", target_bir_lowering=False)
    NT = ntok // 128
    NG = ntok // 512
    x = nc.dram_tensor("x", [ntok, D], F32, kind="ExternalInput").ap()
    wq = nc.dram_tensor("wq", [D, 128], F32, kind="ExternalInput").ap()
    wk = nc.dram_tensor("wk", [D, 128], F32, kind="ExternalInput").ap()
    wv = nc.dram_tensor("wv", [D, 128], F32, kind="ExternalInput").ap()
    g = nc.dram_tensor("g", [128, 8], F32, kind="ExternalInput").ap()
    cst = {}
    for name, shp in (("uneg", [128, 128]), ("oneg", [128, 128]), ("ident", [128, 128]),
                      ("cm01", [128, 4, 512]), ("cmneg", [128, 4, 512])):
        cst[name] = nc.dram_tensor(name, shp, BF16, kind="ExternalInput").ap()
    o = nc.dram_tensor("o", [ntok, 128], BF16, kind="ExternalOutput").ap()

    k = K(nc)
    sb = nc.alloc_sbuf_tensor
    ps = nc.alloc_psum_tensor
    c_sb = {n: sb("c_" + n, list(a.shape), BF16) for n, a in cst.items()}
    c_b = {n: k.buf(n) for n in cst}
    for n in cst:
        k.dma("sp", c_sb[n][:], cst[n], writes=[c_b[n]])
    g_sb = sb("g_sb", [128, 8], F32)
    g_b = k.buf("g")
    k.dma("sp", g_sb[:], g, writes=[g_b])
    wst = sb("wst", [128, 8, 128], F32)
    wst_b = k.buf("wst")
    wb = {}
    wb_b = {}
    for name, w_ap, scl in (("q", wq, 0.125), ("k", wk, 1.0), ("v", wv, 1.0)):
        wb[name] = sb("wb_" + name, [128, 8, 128], BF16)
        wb_b[name] = k.buf("wb" + name)
        k.dma("sp", wst[:], w_ap.rearrange("(k p) n -> p k n", p=128), writes=[wst_b])
        for kk in range(8):
            k.op("dve", lambda e, kk=kk, name=name, scl=scl: e.tensor_scalar(
                wb[name][:, kk, :], wst[:, kk, :], g_sb[:, kk:kk + 1], scl, ALU.mult, ALU.mult),
                reads=[wst_b, g_b], writes=[wb_b[name]])

    QT = sb("QT", [128, ntok], BF16)
    KT = sb("KT", [128, ntok], BF16)
    V = sb("V", [128, NT, 128], BF16)
    QT_b = k.bufs(NG, "QT")
    KT_b = k.bufs(NG, "KT")
    V_b = k.bufs(NG, "V")

    NXB = 3
    xin = [sb("xin%d" % i, [128, D], F32) for i in range(NXB)]
    xin_b = k.bufs(NXB, "xin")
    junk = sb("junk", [128, D], F32)
    junk_b = k.buf("junk")
    ssq = [sb("ssq%d" % i, [128, 1], F32) for i in range(2)]
    ssq_b = k.bufs(2, "ssq")
    rstd = [sb("rstd%d" % i, [128, 1], F32) for i in range(2)]
    rstd_b = k.bufs(2, "rstd")
    xh = [sb("xh%d" % i, [128, D], BF16) for i in range(2)]
    xh_b = k.bufs(2, "xh")
    xT = [sb("xT%d" % i, [128, 8, 512], BF16) for i in range(2)]
    xT_b = k.bufs(2, "xT")
    pT = [ps("pT%d" % i, [128, 8, 128], BF16) for i in range(2)]
    pT_b = k.bufs(2, "pT")
    pA = [ps("pA%d" % i, [128, 512], F32) for i in range(2)]
    pA_b = k.bufs(2, "pA")
    pB = [ps("pB%d" % i, [128, 512], F32) for i in range(2)]
    pB_b = k.bufs(2, "pB")
    pO = [ps("pO%d" % i, [128, 4, 64], F32) for i in range(2)]
    pO_b = k.bufs(2, "pO")

    nproj = 0
    for gi in range(NG):
        xt_i = gi % 2
        for ti in range(4):
            t = gi * 4 + ti
            xi = t % NXB
            si = t % 2
            k.dma("sp", xin[xi][:], x[t * 128:(t + 1) * 128, :], writes=[xin_b[xi]])
            k.op("act", lambda e, xi=xi, si=si: e.activation(
                out=junk[:], in_=xin[xi][:], func=AF.Square, accum_out=ssq[si][:]),
                reads=[xin_b[xi]], writes=[junk_b, ssq_b[si]])
            k.op("act", lambda e, si=si: e.activation(
                out=rstd[si][:], in_=ssq[si][:], func=AF.Sqrt, scale=1.0 / D, bias=EPS),
                reads=[ssq_b[si]], writes=[rstd_b[si]])
            k.op("dve", lambda e, si=si: e.reciprocal(rstd[si][:], rstd[si][:]),
                reads=[rstd_b[si]], writes=[rstd_b[si]])
            k.op("dve", lambda e, xi=xi, si=si: e.tensor_scalar(
                xh[si][:], xin[xi][:], rstd[si][:, 0:1], None, ALU.mult),
                reads=[xin_b[xi], rstd_b[si]], writes=[xh_b[si]])
            for kk in range(8):
                k.op("pe", lambda e, si=si, kk=kk: e.transpose(
                    pT[si][:, kk, :], xh[si][:, kk * 128:(kk + 1) * 128], c_sb["ident"][:]),
                    reads=[xh_b[si], c_b["ident"]], writes=[pT_b[si]])
            k.op("act", lambda e, si=si, xt_i=xt_i, ti=ti: e.activation(
                out=xT[xt_i][:, :, ti * 128:(ti + 1) * 128], in_=pT[si][:], func=AF.Copy),
                reads=[pT_b[si]], writes=[xT_b[xt_i]])
        for name, dst, dst_b in (("q", QT, QT_b), ("k", KT, KT_b)):
            pi = nproj % 2
            nproj += 1
            for kk in range(8):
                k.op("pe", lambda e, pi=pi, kk=kk, name=name, xt_i=xt_i: e.matmul(
                    pA[pi][:], wb[name][:, kk, :], xT[xt_i][:, kk, :], start=(kk == 0), stop=(kk == 7)),
                    reads=[wb_b[name], xT_b[xt_i]], writes=[pA_b[pi]])
            k.op("dve", lambda e, pi=pi, dst=dst, gi=gi: e.tensor_copy(
                dst[:, gi * 512:(gi + 1) * 512], pA[pi][:]),
                reads=[pA_b[pi]], writes=[dst_b[gi]])
        pi = nproj % 2
        nproj += 1
        for ti in range(4):
            for kk in range(8):
                k.op("pe", lambda e, pi=pi, kk=kk, ti=ti, xt_i=xt_i: e.matmul(
                    pA[pi][:, ti * 128:(ti + 1) * 128], xT[xt_i][:, kk, ti * 128:(ti + 1) * 128],
                    wb["v"][:, kk, :], start=(kk == 0), stop=(kk == 7)),
                    reads=[wb_b["v"], xT_b[xt_i]], writes=[pA_b[pi]])
        k.op("dve", lambda e, pi=pi, gi=gi: e.tensor_copy(
            V[:, gi * 4:(gi + 1) * 4, :], pA[pi][:].rearrange("p (a b) -> p a b", a=4)),
            reads=[pA_b[pi]], writes=[V_b[gi]])

    spb = [sb("spb%d" % i, [128, 512], BF16) for i in range(2)]
    spb_b = k.bufs(2, "spb")
    spsum = [sb("spsum%d" % i, [128, 512], BF16) for i in range(2)]
    spsum_b = k.bufs(2, "spsum")
    aT = [sb("aT%d" % i, [128, 512], BF16) for i in range(2)]
    aT_b = k.bufs(2, "aT")
    osb = [sb("osb%d" % i, [128, 4, 128], BF16) for i in range(2)]
    osb_b = k.bufs(2, "osb")
    o_v = o.rearrange("(t p) f -> p t f", p=128)
    ntile = 0
    for G in range(NG):
        oi = G % 2
        for h in range(2):
            hp = slice(64 * h, 64 * h + 64)
            acc_i = (G * 2 + h) % 2
            nkb = 4 * (G + 1)
            qs = slice(512 * G, 512 * G + 512)
            sps = spsum[acc_i]
            sps_b = spsum_b[acc_i]
            for step, kb in enumerate(range(nkb - 1, -1, -1)):
                i2 = ntile % 2
                ntile += 1
                jd = kb - 4 * G
                ks = slice(128 * kb, 128 * kb + 128)
                k.op("pe", lambda e, i2=i2, hp=hp, ks=ks, qs=qs: e.matmul(
                    pA[i2][:], KT[hp, ks], QT[hp, qs], start=True, stop=True),
                    reads=[KT_b[kb // 4], QT_b[G]], writes=[pA_b[i2]])
                k.op("act", lambda e, i2=i2: e.activation(out=spb[i2][:], in_=pA[i2][:], func=AF.Softplus),
                     reads=[pA_b[i2]], writes=[spb_b[i2]])
                if jd >= 0:
                    k.op("dve", lambda e, i2=i2, jd=jd: e.tensor_tensor(
                        spb[i2][:], spb[i2][:], c_sb["cm01"][:, jd, :], ALU.mult),
                        reads=[spb_b[i2], c_b["cm01"]], writes=[spb_b[i2]])
                rd = [KT_b[kb // 4], QT_b[G]]
                k.op("pe", lambda e, i2=i2, hp=hp, ks=ks, qs=qs: e.matmul(
                    pB[i2][:], KT[hp, ks], QT[hp, qs], start=True, stop=False),
                    reads=rd, writes=[pB_b[i2]])
                last_is_u = (step == 0 and jd < 0)
                k.op("pe", lambda e, i2=i2, last_is_u=last_is_u: e.matmul(
                    pB[i2][:], c_sb["uneg"][:], spb[i2][:], start=False, stop=last_is_u),
                    reads=[c_b["uneg"], spb_b[i2]], writes=[pB_b[i2]])
                if step > 0:
                    k.op("pe", lambda e, i2=i2, sps=sps, jd=jd: e.matmul(
                        pB[i2][:], c_sb["oneg"][:], sps[:], start=False, stop=(jd < 0)),
                        reads=[c_b["oneg"], sps_b], writes=[pB_b[i2]])
                if jd >= 0:
                    k.op("pe", lambda e, i2=i2, jd=jd: e.matmul(
                        pB[i2][:], c_sb["ident"][:], c_sb["cmneg"][:, jd, :], start=False, stop=True),
                        reads=[c_b["ident"], c_b["cmneg"]], writes=[pB_b[i2]])
                k.op("act", lambda e, i2=i2: e.activation(out=aT[i2][:], in_=pB[i2][:], func=AF.Exp),
                     reads=[pB_b[i2]], writes=[aT_b[i2]])
                for tc in range(4):
                    k.op("pe", lambda e, i2=i2, tc=tc, kb=kb, h=h, acc_i=acc_i, step=step, nkb=nkb: e.matmul(
                        pO[acc_i][:, tc, :], aT[i2][:, tc * 128:(tc + 1) * 128],
                        V[:, kb, 64 * h:64 * h + 64], start=(step == 0 and tc == 0), stop=(step == nkb - 1)),
                        reads=[aT_b[i2], V_b[kb // 4]], writes=[pO_b[acc_i]])
                if step < nkb - 1:
                    if step == 0:
                        k.op("pool", lambda e, i2=i2, sps=sps: e.tensor_copy(sps[:], spb[i2][:]),
                             reads=[spb_b[i2]], writes=[sps_b])
                    else:
                        k.op("pool", lambda e, i2=i2, sps=sps: e.tensor_tensor(
                            sps[:], sps[:], spb[i2][:], ALU.add),
                            reads=[spb_b[i2], sps_b], writes=[sps_b])
            k.op("dve", lambda e, acc_i=acc_i, oi=oi, h=h: e.tensor_copy(
                osb[oi][:, :, 64 * h:64 * h + 64], pO[acc_i][:]),
                reads=[pO_b[acc_i]], writes=[osb_b[oi]])
        k.dma("sp", o_v[:, 4 * G:4 * G + 4, :], osb[oi][:], reads=[osb_b[oi]])
    k.wait("sp", osb_b)
    k.emit()
    return nc


def run_L1(x2d, sb_w_qkv, g00, ntok=S):
    nc = build_L1(ntok)
    cst = _consts_attn()
    gl = np.ascontiguousarray(g00.reshape(8, 128).T)
    in_maps = []
    for c in range(NCORE):
        m = {"x": x2d, "g": gl}
        for i, nm in enumerate(("wq", "wk", "wv")):
            m[nm] = np.ascontiguousarray(sb_w_qkv[:, i * 1024 + 128 * c: i * 1024 + 128 * c + 128])
        m.update(cst)
        in_maps.append(m)
    res = run_bass_kernel_spmd(nc, in_maps, core_ids=list(range(NCORE)))
    return np.concatenate([res.results[c]["o"] for c in range(NCORE)], axis=1)


def build_tok(final, NTOK=2048):
    nc = bass.Bass("TRN2", target_bir_lowering=False)
    NT = NTOK // 128
    NTG = NTOK // 512
    hin = nc.dram_tensor("hin", [NTOK, D], F32, kind="ExternalInput").ap()
    oT = nc.dram_tensor("oT", [D, NTOK], BF16, kind="ExternalInput").ap()
    wo = nc.dram_tensor("wo", [D, D], F32, kind="ExternalInput").ap()
    w1 = nc.dram_tensor("w1", [D, 4 * D], F32, kind="ExternalInput").ap()
    w2 = nc.dram_tensor("w2", [4 * D, D], F32, kind="ExternalInput").ap()
    g = nc.dram_tensor("g", [128, 8], F32, kind="ExternalInput").ap()
    ident = nc.dram_tensor("ident", [128, 128], BF16, kind="ExternalInput").ap()
    if final:
        gfin = nc.dram_tensor("gfin", [128, D], F32, kind="ExternalInput").ap()
        y = nc.dram_tensor("y", [NTOK, D], F32, kind="ExternalOutput").ap()
    else:
        hout = nc.dram_tensor("hout", [NTOK, D], F32, kind="ExternalOutput").ap()
        xT_out = nc.dram_tensor("xT", [D, NTOK], BF16, kind="ExternalOutput").ap()

    k = K(nc)
    sb = nc.alloc_sbuf_tensor
    ps = nc.alloc_psum_tensor
    id_sb = sb("id_sb", [128, 128], BF16)
    id_b = k.buf("id")
    k.dma("sp", id_sb[:], ident, writes=[id_b])
    g_sb = sb("g_sb", [128, 8], F32)
    g_b = k.buf("g")
    k.dma("sp", g_sb[:], g, writes=[g_b])
    if final:
        gf_sb = sb("gf_sb", [128, D], F32)
        gf_b = k.buf("gf")
        k.dma("sp", gf_sb[:], gfin, writes=[gf_b])

    h = sb("h", [128, NT, D], F32)
    h_b = k.bufs(NT, "h")
    actT = sb("actT", [128, 8, NTOK], BF16)
    actT_b = k.bufs(NT, "actT")
    wstA = sb("wstA", [128, 4, D], F32)
    wstA_b = k.buf("wstA")
    wstB = sb("wstB", [128, 8, 512], F32)
    wstB_b = k.buf("wstB")
    wo_sb = sb("wo_sb", [128, 8, D], BF16)
    wo_b = k.bufs(2, "wo")
    w1b = [sb("w1b%d" % i, [128, 8, 512], BF16) for i in range(2)]
    w1b_b = k.bufs(2, "w1b")
    w2b = [sb("w2b%d" % i, [128, 4, D], BF16) for i in range(2)]
    w2b_b = k.bufs(2, "w2b")
    h1T = [sb("h1T%d" % i, [128, 4, 512], BF16) for i in range(2)]
    h1T_b = [k.bufs(4, "h1T%d_" % i) for i in range(2)]
    rl = [sb("rl%d" % i, [128, 512], F32) for i in range(2)]
    rl_b = k.bufs(2, "rl")
    junk = sb("junk", [128, D], BF16)
    junk_b = k.buf("junk")
    ssq = [sb("ssq%d" % i, [128, 1], F32) for i in range(2)]
    ssq_b = k.bufs(2, "ssq")
    rstd = [sb("rstd%d" % i, [128, 1], F32) for i in range(2)]
    rstd_b = k.bufs(2, "rstd")
    xh = [sb("xh%d" % i, [128, D], BF16) for i in range(2)]
    xh_b = k.bufs(2, "xh")
    pT = [ps("pT%d" % i, [128, 8, 128], BF16) for i in range(2)]
    pT_b = k.bufs(2, "pT")
    pA = [ps("pA%d" % i, [128, 512], F32) for i in range(4)]
    pA_b = k.bufs(4, "pA")
    npa = [0]

    def next_pa():
        i = npa[0] % 4
        npa[0] += 1
        return i

    hin_v = hin.rearrange("(t p) f -> p t f", p=128)
    for t in range(NT):
        k.dma("sp", h[:, t, :], hin_v[:, t, :], writes=[h_b[t]])
    oT_v = oT.rearrange("(k p) t -> p k t", p=128)
    for tg in range(NTG):
        k.dma("pool", actT[:, :, tg * 512:(tg + 1) * 512], oT_v[:, :, tg * 512:(tg + 1) * 512],
              writes=actT_b[tg * 4:(tg + 1) * 4], sembuf=actT_b[tg * 4])
    wo_v = wo.rearrange("(k p) n -> p k n", p=128)
    for half in range(2):
        k.dma("sp", wstA[:], wo_v[:, half * 4:(half + 1) * 4, :], writes=[wstA_b])
        k.op("dve", lambda e, half=half: e.tensor_copy(wo_sb[:, half * 4:(half + 1) * 4, :], wstA[:]),
             reads=[wstA_b], writes=[wo_b[half]])

    for t in range(NT):
        for nh in range(2):
            pi = next_pa()
            for kk in range(8):
                k.op("pe", lambda e, pi=pi, kk=kk, t=t, nh=nh: e.matmul(
                    pA[pi][:], actT[:, kk, t * 128:(t + 1) * 128], wo_sb[:, kk, nh * 512:(nh + 1) * 512],
                    start=(kk == 0), stop=(kk == 7)),
                    reads=[actT_b[t], wo_b[kk // 4]], writes=[pA_b[pi]])
            k.op("dve", lambda e, pi=pi, t=t, nh=nh: e.tensor_tensor(
                h[:, t, nh * 512:(nh + 1) * 512], h[:, t, nh * 512:(nh + 1) * 512], pA[pi][:], ALU.add),
                reads=[pA_b[pi], h_b[t]], writes=[h_b[t]])

    def norm_T(t):
        si = t % 2
        k.op("act", lambda e, si=si, t=t: e.activation(
            out=junk[:], in_=h[:, t, :], func=AF.Square, accum_out=ssq[si][:]),
            reads=[h_b[t]], writes=[junk_b, ssq_b[si]])
        k.op("act", lambda e, si=si: e.activation(
            out=rstd[si][:], in_=ssq[si][:], func=AF.Sqrt, scale=1.0 / D, bias=EPS),
            reads=[ssq_b[si]], writes=[rstd_b[si]])
        k.op("dve", lambda e, si=si: e.reciprocal(rstd[si][:], rstd[si][:]),
             reads=[rstd_b[si]], writes=[rstd_b[si]])
        k.op("dve", lambda e, si=si, t=t: e.tensor_scalar(
            xh[si][:], h[:, t, :], rstd[si][:, 0:1], None, ALU.mult),
            reads=[h_b[t], rstd_b[si]], writes=[xh_b[si]])
        for kk in range(8):
            k.op("pe", lambda e, si=si, kk=kk: e.transpose(
                pT[si][:, kk, :], xh[si][:, kk * 128:(kk + 1) * 128], id_sb[:]),
                reads=[xh_b[si], id_b], writes=[pT_b[si]])
        k.op("act", lambda e, si=si, t=t: e.activation(
            out=actT[:, :, t * 128:(t + 1) * 128], in_=pT[si][:], func=AF.Copy),
            reads=[pT_b[si]], writes=[actT_b[t]])

    for t in range(NT):
        norm_T(t)

    w1_v = w1.rearrange("(k p) f -> p k f", p=128)
    w2_v = w2.rearrange("(c p) n -> p c n", p=128)
    for fg in range(8):
        wi = fg % 2
        k.dma("sp", wstB[:], w1_v[:, :, fg * 512:(fg + 1) * 512], writes=[wstB_b])
        for kk in range(8):
            k.op("pool", lambda e, wi=wi, kk=kk: e.tensor_scalar(
                w1b[wi][:, kk, :], wstB[:, kk, :], g_sb[:, kk:kk + 1], None, ALU.mult),
                reads=[wstB_b, g_b], writes=[w1b_b[wi]])
        k.dma("sp", wstA[:], w2_v[:, fg * 4:(fg + 1) * 4, :], writes=[wstA_b])
        k.op("pool", lambda e, wi=wi: e.tensor_copy(w2b[wi][:], wstA[:]),
             reads=[wstA_b], writes=[w2b_b[wi]])
        for tg in range(NTG):
            hi = (fg * NTG + tg) % 2
            for fc in range(4):
                pi = next_pa()
                for kk in range(8):
                    k.op("pe", lambda e, pi=pi, kk=kk, fc=fc, tg=tg, wi=wi: e.matmul(
                        pA[pi][:], w1b[wi][:, kk, fc * 128:(fc + 1) * 128],
                        actT[:, kk, tg * 512:(tg + 1) * 512], start=(kk == 0), stop=(kk == 7)),
                        reads=[w1b_b[wi]] + actT_b[tg * 4:(tg + 1) * 4], writes=[pA_b[pi]])
                ri = fc % 2
                k.op("act", lambda e, pi=pi, ri=ri: e.activation(out=rl[ri][:], in_=pA[pi][:], func=AF.Relu),
                     reads=[pA_b[pi]], writes=[rl_b[ri]])
                k.op("dve", lambda e, ri=ri, hi=hi, fc=fc: e.tensor_tensor(
                    h1T[hi][:, fc, :], rl[ri][:], rl[ri][:], ALU.mult),
                    reads=[rl_b[ri]], writes=[h1T_b[hi][fc]])
            for tt in range(4):
                t = tg * 4 + tt
                for nh in range(2):
                    pi = next_pa()
                    for fc in range(4):
                        k.op("pe", lambda e, pi=pi, fc=fc, tt=tt, nh=nh, hi=hi, wi=wi: e.matmul(
                            pA[pi][:], h1T[hi][:, fc, tt * 128:(tt + 1) * 128],
                            w2b[wi][:, fc, nh * 512:(nh + 1) * 512], start=(fc == 0), stop=(fc == 3)),
                            reads=[h1T_b[hi][fc], w2b_b[wi]], writes=[pA_b[pi]])
                    k.op("dve", lambda e, pi=pi, t=t, nh=nh: e.tensor_tensor(
                        h[:, t, nh * 512:(nh + 1) * 512], h[:, t, nh * 512:(nh + 1) * 512], pA[pi][:], ALU.add),
                        reads=[pA_b[pi], h_b[t]], writes=[h_b[t]])

    if final:
        y_v = y.rearrange("(t p) f -> p t f", p=128)
        yt = [sb("yt%d" % i, [128, D], F32) for i in range(2)]
        yt_b = k.bufs(2, "yt")
        for t in range(NT):
            si = t % 2
            k.op("act", lambda e, si=si, t=t: e.activation(
                out=junk[:], in_=h[:, t, :], func=AF.Square, accum_out=ssq[si][:]),
                reads=[h_b[t]], writes=[junk_b, ssq_b[si]])
            k.op("act", lambda e, si=si: e.activation(
                out=rstd[si][:], in_=ssq[si][:], func=AF.Sqrt, scale=1.0 / D, bias=EPS),
                reads=[ssq_b[si]], writes=[rstd_b[si]])
            k.op("dve", lambda e, si=si: e.reciprocal(rstd[si][:], rstd[si][:]),
                 reads=[rstd_b[si]], writes=[rstd_b[si]])
            k.op("dve", lambda e, si=si, t=t: e.scalar_tensor_tensor(
                yt[si][:], h[:, t, :], rstd[si][:, 0:1], gf_sb[:], ALU.mult, ALU.mult),
                reads=[h_b[t], rstd_b[si], gf_b], writes=[yt_b[si]])
            k.dma("sp", y_v[:, t, :], yt[si][:], reads=[yt_b[si]])
        k.wait("sp", yt_b)
    else:
        ho_v = hout.rearrange("(t p) f -> p t f", p=128)
        xTo_v = xT_out.rearrange("(k p) t -> p k t", p=128)
        for t in range(NT):
            k.dma("sp", ho_v[:, t, :], h[:, t, :], reads=[h_b[t]])
            norm_T(t)
        for tg in range(NTG):
            k.dma("sp", xTo_v[:, :, tg * 512:(tg + 1) * 512], actT[:, :, tg * 512:(tg + 1) * 512],
                  reads=actT_b[tg * 4:(tg + 1) * 4], sembuf=actT_b[tg * 4])
        k.wait("sp", h_b + actT_b)
    k.emit()
    return nc


def _g_layout(gv):
    return np.ascontiguousarray(gv.reshape(8, 128).T)


def run_tok(final, hin_full, o_full, wo, w1, w2, g_mlp, gfin=None):
    nc = build_tok(final)
    ident = _bf(np.eye(128))
    in_maps = []
    for c in range(NCORE):
        sl = slice(2048 * c, 2048 * (c + 1))
        m = {"hin": np.ascontiguousarray(hin_full[sl]),
             "oT": np.ascontiguousarray(o_full[sl].T),
             "wo": wo, "w1": w1, "w2": w2, "g": _g_layout(g_mlp), "ident": ident}
        if final:
            m["gfin"] = np.ascontiguousarray(np.broadcast_to(gfin[None, :], (128, D)))
        in_maps.append(m)
    res = run_bass_kernel_spmd(nc, in_maps, core_ids=list(range(NCORE)))
    if final:
        return np.concatenate([res.results[c]["y"] for c in range(NCORE)], axis=0)
    h = np.concatenate([res.results[c]["hout"] for c in range(NCORE)], axis=0)
    xT = np.concatenate([res.results[c]["xT"] for c in range(NCORE)], axis=1)
    return h, xT


def _consts_nsa(ntok=S):
    c = {}
    c["ident"] = _bf(np.eye(128))
    sl = np.arange(128)[:, None]
    tl = np.arange(512)[None, :]
    cmi = np.stack([(sl + 128 * j <= tl) for j in range(4)]).astype(np.float32)
    c["cmneg"] = _bf(((cmi - 1.0) * MASKV).transpose(1, 0, 2))
    wm = np.stack([(sl + 128 * j > tl) for j in range(4)]).astype(np.float32)
    c["wmneg"] = _bf(((wm - 1.0) * MASKV).transpose(1, 0, 2))
    cp = np.stack([(16 * sl + 31 - 512 * r <= tl) for r in range(5)]).astype(np.float32)
    c["cpneg"] = _bf(((cp - 1.0) * MASKV).transpose(1, 0, 2))
    m = np.arange(128)[:, None]
    cc = np.arange(8192)[None, :]
    c["eall"] = _bf(np.where(cc // 64 == m, MASKV, 0.0))
    n = (np.arange(8)[None, :, None] * 128 + np.arange(128)[:, None, None])
    mm = np.arange(256)[None, None, :]
    ov = ((n >= 4 * mm - 1) & (n <= 4 * mm + 3)).astype(np.float32)
    c["ovl1"] = _bf(np.concatenate([np.ones((128, 8, 1), np.float32), ov], axis=2))
    half = 8
    inv_freq = (np.float32(500000.0) ** (-np.arange(half, dtype=np.float32) * np.float32(2.0) / np.float32(16)))
    ang = np.arange(ntok, dtype=np.float32)[:, None] * inv_freq[None, :].astype(np.float32)
    cs = np.cos(ang).astype(np.float32).T
    sn = np.sin(ang).astype(np.float32).T
    cos64 = np.ones((64, ntok), np.float32)
    sin64 = np.zeros((64, ntok), np.float32)
    cos64[0:8] = cs
    cos64[8:16] = cs
    sin64[0:8] = -sn
    sin64[8:16] = sn
    c["cosT"] = np.ascontiguousarray(np.concatenate([cos64, cos64], 0))
    c["sinT"] = np.ascontiguousarray(np.concatenate([sin64, sin64], 0))
    tq = np.arange(128)
    fc = np.zeros((128, 2), np.float32)
    fc[:, 0] = np.where(tq < 64, 4e6, 0.0)
    fc[:, 1] = np.where(tq < 64, -1.0, 3e6)
    c["fc"] = fc
    return c


def build_L3(ntok=S, dbg=False):
    nc = bass.Bass("TRN2", target_bir_lowering=False)
    NT = ntok // 128
    NG = ntok // 512
    NCMP = (ntok - 32) // 16 + 1
    NJ = (NCMP + 127) // 128
    NNG = (NCMP + 511) // 512
    di = lambda name, shp, dt=F32: nc.dram_tensor(name, shp, dt, kind="ExternalInput").ap()
    xT = di("xT", [D, ntok], BF16)
    w_kvr = di("w_kvr", [D, 128])
    w_ksw = di("w_ksw", [D, 128])
    w_kswp = di("w_kswp", [D, 128])
    w_vsw = di("w_vsw", [D, 128])
    w_qg = di("w_qg", [D, 256])
    w_q1 = di("w_q1", [D, 128])
    w_q1p = di("w_q1p", [D, 128])
    w_q2 = di("w_q2", [D, 128])
    w_q2p = di("w_q2p", [D, 128])
    w_g = di("w_g", [D, 6])
    gb = di("gb", [1, 6])
    gkv = di("gkv", [128, 8])
    gq = di("gq", [128, 8])
    cw1 = di("cw1", [128, 32, 256])
    cw2 = di("cw2", [128, 2, 2, 64])
    cposT = di("cposT", [128, 32])
    cshape = {"ident": [128, 128], "cmneg": [128, 4, 512], "wmneg": [128, 4, 512], "cpneg": [128, 5, 512],
              "eall": [128, 8192], "ovl1": [128, 8, 257]}
    cst = {n: di(n, s, BF16) for n, s in cshape.items()}
    cosT = di("cosT", [128, ntok])
    sinT = di("sinT", [128, ntok])
    fc = di("fc", [128, 2])
    o = nc.dram_tensor("o", [ntok, 128], BF16, kind="ExternalOutput").ap()

    k = K(nc)
    sb = nc.alloc_sbuf_tensor
    ps = nc.alloc_psum_tensor
    REGION = 75776
    reg_base = (nc.sbuf_top - REGION) // 64 * 64
    cur = {2: reg_base, 3: reg_base}

    def sbr(phase, name, shape, dtype):
        n = 1
        for d_ in shape[1:]:
            n *= d_
        nbytes = n * (4 if dtype == F32 else 2)
        off = cur[phase]
        cur[phase] = off + (nbytes + 63) // 64 * 64
        assert cur[phase] <= reg_base + REGION, (name, cur[phase] - reg_base)
        return nc.alloc_sbuf_tensor_at(name, shape, dtype, offset=off)

    bar = [None]

    def b3(name=""):
        b = k.buf(name)
        b.w = bar[0]
        return b

    c_sb = {"ident": sb("c_ident", cshape["ident"], BF16)}
    c_b = {"ident": k.buf("ident")}
    k.dma("sp", c_sb["ident"][:], cst["ident"], writes=[c_b["ident"]])
    fc_sb = sb("fc_sb", [128, 2], F32)
    fc_b = k.buf("fc")
    k.dma("sp", fc_sb[:], fc, writes=[fc_b])
    gkv_sb = sb("gkv_sb", [128, 8], F32)
    gq_sb = sb("gq_sb", [128, 8], F32)
    gkv_b = k.buf()
    gq_b = k.buf()
    k.dma("sp", gkv_sb[:], gkv, writes=[gkv_b])
    k.dma("sp", gq_sb[:], gq, writes=[gq_b])
    gb_sb = sb("gb_sb", [1, 6], F32)
    gb_b = k.buf()
    k.dma("sp", gb_sb[:], gb, writes=[gb_b])
    gbb = sb("gbb", [1, 6], BF16)
    gbb_b = k.buf()
    k.op("dve", lambda e: e.tensor_copy(gbb[:], gb_sb[:]), reads=[gb_b], writes=[gbb_b])
    ones1 = sb("ones1", [1, 128], BF16)
    ones1_b = k.buf()
    k.op("dve", lambda e: e.memset(ones1[:], 1.0), writes=[ones1_b])

    wst = sb("wst", [128, 8, 256], F32)
    wst_b = k.buf("wst")
    W = {}
    W_b = {}

    def load_w(name, ap, ncol, gsb, gbuf, scl):
        W[name] = sb("W_" + name, [128, 8, ncol], BF16)
        W_b[name] = k.buf("W" + name)
        k.dma("sp", wst[:, :, 0:ncol], ap.rearrange("(k p) n -> p k n", p=128), writes=[wst_b])
        for kk in range(8):
            k.op("dve", lambda e, kk=kk, name=name, scl=scl, gsb=gsb, ncol=ncol: e.tensor_scalar(
                W[name][:, kk, :], wst[:, kk, 0:ncol], gsb[:, kk:kk + 1], scl, ALU.mult, ALU.mult),
                reads=[wst_b, gbuf], writes=[W_b[name]])

    load_w("kvr", w_kvr, 128, gkv_sb, gkv_b, 1.0)
    load_w("ksw", w_ksw, 128, gkv_sb, gkv_b, 1.0)
    load_w("kswp", w_kswp, 128, gkv_sb, gkv_b, 1.0)
    load_w("vsw", w_vsw, 128, gkv_sb, gkv_b, 1.0)
    load_w("qg", w_qg, 256, gq_sb, gq_b, 0.125)
    load_w("q1", w_q1, 128, gq_sb, gq_b, 0.125)
    load_w("q1p", w_q1p, 128, gq_sb, gq_b, 0.125)
    load_w("q2", w_q2, 128, gq_sb, gq_b, 0.125)
    load_w("q2p", w_q2p, 128, gq_sb, gq_b, 0.125)
    load_w("g", w_g, 6, gq_sb, gq_b, 1.0)

    cw1st = sbr(2, "cw1st", [128, 8, 256], F32)
    cw1st_b = k.buf()
    cw1b = sbr(2, "cw1b", [128, 32, 256], BF16)
    cw1b_b = k.bufs(2, "cw1b")
    for qt in range(4):
        k.dma("sp", cw1st[:], cw1[:, qt * 8:(qt + 1) * 8, :], writes=[cw1st_b])
        k.op("dve", lambda e, qt=qt: e.tensor_copy(cw1b[:, qt * 8:(qt + 1) * 8, :], cw1st[:]),
             reads=[cw1st_b], writes=[cw1b_b[qt // 2]])
    cw2st = sb("cw2st", [128, 2, 2, 64], F32)
    cw2st_b = k.buf()
    k.dma("sp", cw2st[:], cw2, writes=[cw2st_b])
    cw2k = sb("cw2k", [128, 2, 128], BF16)
    cw2v = sb("cw2v", [128, 2, 64], BF16)
    cw2_b = k.buf()
    k.op("dve", lambda e: e.tensor_copy(cw2k[:, :, 0:64], cw2st[:, :, 0, :]), reads=[cw2st_b], writes=[cw2_b])
    k.op("dve", lambda e: e.tensor_copy(cw2k[:, :, 64:128], cw2st[:, :, 0, :]), reads=[cw2st_b], writes=[cw2_b])
    k.op("dve", lambda e: e.tensor_copy(cw2v[:], cw2st[:, :, 1, :]), reads=[cw2st_b], writes=[cw2_b])
    cpos_st = sb("cpos_st", [128, 32], F32)
    cpos_b = k.buf()
    k.dma("sp", cpos_st[:], cposT, writes=[cpos_b])
    cposb = sb("cposb", [128, 32], BF16)
    cposb_b = k.buf()
    k.op("dve", lambda e: e.tensor_copy(cposb[:], cpos_st[:]), reads=[cpos_b], writes=[cposb_b])

    RAW = sbr(2, "RAW", [128, ntok], BF16)
    RAW_b = k.bufs(NG, "RAW")
    KSW = sb("KSW", [128, ntok], BF16)
    KSW_b = k.bufs(NG, "KSW")
    VSW = sb("VSW", [128, NT, 132], BF16)
    VSW_b = k.bufs(NG, "VSW")
    vones_b = k.buf()
    k.op("pool", lambda e: e.memset(VSW[:, :, 64:65], 1.0), writes=[vones_b] + VSW_b)
    k.op("pool", lambda e: e.memset(VSW[:, :, 130:131], 1.0), writes=[vones_b] + VSW_b)

    xg = [sb("xg%d" % i, [128, 8, 512], BF16) for i in range(1)]
    xg_b = k.bufs(1, "xg")
    cs_sb = [sb("cs%d" % i, [128, 512], F32) for i in range(1)]
    sn_sb = [sb("sn%d" % i, [128, 512], F32) for i in range(1)]
    cs_b = k.bufs(1, "cs")
    sn_b = k.bufs(1, "sn")
    t1 = [sb("t1_%d" % i, [128, 512], F32) for i in range(1)]
    t2 = [sb("t2_%d" % i, [128, 512], F32) for i in range(1)]
    t1_b = k.bufs(1, "t1")
    t2_b = k.bufs(1, "t2")
    pS = [ps("pS%d" % i, [128, 512], F32) for i in range(2)]
    pS_b = k.bufs(2, "pS")
    nps = [0]

    def next_ps():
        i = nps[0] % 2
        nps[0] += 1
        return i

    xT_v = xT.rearrange("(k p) t -> p k t", p=128)

    def load_group(G, cnt):
        i = 0
        k.dma("sp", xg[i][:], xT_v[:, :, G * 512:(G + 1) * 512], writes=[xg_b[i]])
        k.dma("sp", cs_sb[i][:], cosT[:, G * 512:(G + 1) * 512], writes=[cs_b[i]])
        k.dma("sp", sn_sb[i][:], sinT[:, G * 512:(G + 1) * 512], writes=[sn_b[i]])
        return i

    def proj(wname, i, col0=0, ncol=128):
        pi = next_ps()
        for kk in range(8):
            k.op("pe", lambda e, pi=pi, kk=kk, wname=wname, i=i, col0=col0, ncol=ncol: e.matmul(
                pS[pi][0:ncol, :], W[wname][:, kk, col0:col0 + ncol], xg[i][:, kk, :],
                start=(kk == 0), stop=(kk == 7)),
                reads=[W_b[wname], xg_b[i]], writes=[pS_b[pi]])
        return pi

    def rope_to(dst_ap, dst_bufs, wname, wpname, i, ti):
        ti = 0
        pa = proj(wname, i)
        pb = proj(wpname, i)
        k.op("dve", lambda e, pa=pa, i=i, ti=ti: e.tensor_tensor(t1[ti][:], pS[pa][:], cs_sb[i][:], ALU.mult),
             reads=[pS_b[pa], cs_b[i]], writes=[t1_b[ti]])
        k.op("dve", lambda e, pb=pb, i=i, ti=ti: e.tensor_tensor(t2[ti][:], pS[pb][:], sn_sb[i][:], ALU.mult),
             reads=[pS_b[pb], sn_b[i]], writes=[t2_b[ti]])
        k.op("pool", lambda e, ti=ti, dst_ap=dst_ap: e.tensor_tensor(dst_ap, t1[ti][:], t2[ti][:], ALU.add),
             reads=[t1_b[ti], t2_b[ti]], writes=dst_bufs)

    cnt = 0
    for G in range(NG):
        i = load_group(G, cnt)
        cnt += 1
        pi = proj("kvr", i)
        k.op("act", lambda e, pi=pi, G=G: e.activation(out=RAW[:, G * 512:(G + 1) * 512], in_=pS[pi][:], func=AF.Copy),
             reads=[pS_b[pi]], writes=[RAW_b[G]])
        rope_to(KSW[:, G * 512:(G + 1) * 512], [KSW_b[G]], "ksw", "kswp", i, G % 2)
        pi = next_ps()
        for tt in range(4):
            for kk in range(8):
                k.op("pe", lambda e, pi=pi, kk=kk, tt=tt, i=i: e.matmul(
                    pS[pi][:, tt * 128:(tt + 1) * 128], xg[i][:, kk, tt * 128:(tt + 1) * 128], W["vsw"][:, kk, :],
                    start=(kk == 0), stop=(kk == 7)),
                    reads=[W_b["vsw"], xg_b[i]], writes=[pS_b[pi]])
        pv = pS[pi][:].rearrange("p (a b) -> p a b", a=4)
        k.op("act", lambda e, pv=pv, G=G: e.activation(out=VSW[:, G * 4:(G + 1) * 4, 0:64], in_=pv[:, :, 0:64], func=AF.Copy),
             reads=[pS_b[pi]], writes=[VSW_b[G]])
        k.op("act", lambda e, pv=pv, G=G: e.activation(out=VSW[:, G * 4:(G + 1) * 4, 66:130], in_=pv[:, :, 64:128], func=AF.Copy),
             reads=[pS_b[pi]], writes=[VSW_b[G]])

    KCT = sb("KCT", [128, NJ * 128], BF16)
    KCT_b = k.buf("KCT")
    RC = sb("RC", [128, NJ, 322], BF16)[:, :, 0:321]
    RC_b = k.buf("RC")
    k.op("pool", lambda e: e.memset(KCT[:], 0.0), writes=[KCT_b])
    k.op("pool", lambda e: e.memset(RC[:, :, 0:64], 0.0), writes=[RC_b])
    c_sb["ovl1"] = sb("c_ovl1", cshape["ovl1"], BF16)
    c_b["ovl1"] = k.buf("ovl1")
    k.dma("sp", c_sb["ovl1"][:], cst["ovl1"], writes=[c_b["ovl1"]])
    k.op("pool", lambda e: e.tensor_copy(RC[:, :, 64:321], c_sb["ovl1"][:, 0:NJ, :]), reads=[c_b["ovl1"]], writes=[RC_b])
    posb = sb("posb", [128, 2, 2], F32)
    posb_b = k.buf()
    pM = ps("pM", [128, 512], F32)
    pP = pM[:, 0:4]
    pP_b = k.buf()
    for c2 in range(2):
        for ch in range(2):
            for l in range(32):
                k.op("pe", lambda e, c2=c2, ch=ch, l=l: e.matmul(
                    pP[:, c2 * 2 + ch: c2 * 2 + ch + 1], cw1b[64 * c2:64 * c2 + 64, l, ch * 128:(ch + 1) * 128],
                    cposb[64 * c2:64 * c2 + 64, l:l + 1], start=(l == 0), stop=(l == 31)),
                    reads=[cw1b_b[l // 16], cposb_b], writes=[pP_b])
    for c2 in range(2):
        for ch in range(2):
            k.op("dve", lambda e, c2=c2, ch=ch: e.tensor_copy(posb[:, ch, c2:c2 + 1], pP[:, c2 * 2 + ch: c2 * 2 + ch + 1]),
                 reads=[pP_b], writes=[posb_b])
    hidT = sbr(2, "hidT", [128, 2, 2, 1024], BF16)
    hid_b = [[k.buf() for _ in range(2)] for _ in range(2)]
    k.op("pool", lambda e: e.memset(hidT[:], 0.0), writes=[hid_b[0][0], hid_b[0][1], hid_b[1][0], hid_b[1][1]])
    gu = [sbr(2, "gu%d" % i, [128, 512], F32) for i in range(2)]
    gt = [sbr(2, "gt%d" % i, [128, 512], F32) for i in range(2)]
    gu_b = k.bufs(2, "gu")
    gt_b = k.bufs(2, "gt")
    gi = 0
    for c2 in range(2):
        cp = slice(64 * c2, 64 * c2 + 64)
        for ch in range(2):
            for ng in range(NNG):
                n0 = ng * 512
                nn = min(512, NCMP - n0)
                pi = next_ps()
                for l in range(32):
                    k.op("pe", lambda e, pi=pi, l=l, cp=cp, ch=ch, n0=n0, nn=nn: e.matmul(
                        pS[pi][:, 0:nn], cw1b[cp, l, ch * 128:(ch + 1) * 128],
                        RAW[cp, 16 * n0 + l: 16 * n0 + l + 16 * (nn - 1) + 1: 16],
                        start=(l == 0), stop=(l == 31)),
                        reads=[cw1b_b[l // 16]] + RAW_b, writes=[pS_b[pi]])
                ui = gi % 2
                gi += 1
                k.op("act", lambda e, pi=pi, ui=ui, nn=nn, ch=ch, c2=c2: e.activation(
                    out=gu[ui][:, 0:nn], in_=pS[pi][:, 0:nn], func=AF.Identity, bias=posb[:, ch, c2:c2 + 1]),
                    reads=[pS_b[pi], posb_b], writes=[gu_b[ui]])
                k.op("dve", lambda e, ui=ui, nn=nn: e.tensor_tensor(gt[ui][:, 0:nn], gu[ui][:, 0:nn], gu[ui][:, 0:nn], ALU.mult),
                     reads=[gu_b[ui]], writes=[gt_b[ui]])
                k.op("dve", lambda e, ui=ui, nn=nn: e.tensor_scalar(gt[ui][:, 0:nn], gt[ui][:, 0:nn], 0.044715, 1.0, ALU.mult, ALU.add),
                     reads=[gt_b[ui]], writes=[gt_b[ui]])
                k.op("dve", lambda e, ui=ui, nn=nn: e.tensor_tensor(gt[ui][:, 0:nn], gt[ui][:, 0:nn], gu[ui][:, 0:nn], ALU.mult),
                     reads=[gt_b[ui], gu_b[ui]], writes=[gt_b[ui]])
                k.op("act", lambda e, ui=ui, nn=nn: e.activation(
                    out=gt[ui][:, 0:nn], in_=gt[ui][:, 0:nn], func=AF.Sigmoid, scale=1.5957691216057308),
                    reads=[gt_b[ui]], writes=[gt_b[ui]])
                k.op("dve", lambda e, ui=ui, nn=nn, c2=c2, ch=ch, n0=n0: e.tensor_tensor(
                    hidT[:, c2, ch, n0:n0 + nn], gt[ui][:, 0:nn], gu[ui][:, 0:nn], ALU.mult),
                    reads=[gt_b[ui], gu_b[ui]], writes=[hid_b[c2][ch]])
    for ng in range(NNG):
        n0 = ng * 512
        nn = min(512, NCMP - n0)
        pi = next_ps()
        for ch in range(2):
            k.op("pe", lambda e, pi=pi, ch=ch, n0=n0, nn=nn: e.matmul(
                pS[pi][:, 0:nn], cw2k[:, ch, :], hidT[:, 0, ch, n0:n0 + nn], start=(ch == 0), stop=(ch == 1)),
                reads=[cw2_b, hid_b[0][ch]], writes=[pS_b[pi]])
        k.op("act", lambda e, pi=pi, n0=n0, nn=nn: e.activation(out=KCT[:, n0:n0 + nn], in_=pS[pi][:, 0:nn], func=AF.Copy),
             reads=[pS_b[pi]], writes=[KCT_b])
    for j in range(NJ):
        nn = min(128, NCMP - 128 * j)
        pi = next_ps()
        for ch in range(2):
            k.op("pe", lambda e, pi=pi, ch=ch, j=j, nn=nn: e.matmul(
                pS[pi][0:nn, 0:64], hidT[:, 1, ch, 128 * j:128 * j + nn], cw2v[:, ch, :], start=(ch == 0), stop=(ch == 1)),
                reads=[cw2_b, hid_b[1][ch]], writes=[pS_b[pi]])
        k.op("act", lambda e, pi=pi, j=j, nn=nn: e.activation(out=RC[0:nn, j, 0:64], in_=pS[pi][0:nn, 0:64], func=AF.Copy),
             reads=[pS_b[pi]], writes=[RC_b])

    bar[0] = ("c", "act", len(k.ops["act"]) - 1)
    for n in ("eall", "cmneg", "wmneg", "cpneg"):
        c_sb[n] = sbr(3, "c_" + n, cshape[n], BF16)
        c_b[n] = b3(n)
        k.dma("sp", c_sb[n][:], cst[n], writes=[c_b[n]])
    QC = [sbr(3, "QC%d" % i, [128, 2, 512], BF16) for i in range(2)]
    QC_b = [b3() for _ in range(2)]
    QR = [sbr(3, "QR%d" % i, [128, 2, 512], BF16) for i in range(2)]
    QR_b = [[b3() for _ in range(2)] for i in range(2)]
    GT = [sb("GT%d" % i, [128, 4, 6], F32) for i in range(2)]
    GT_b = k.bufs(2, "GT")
    pG = pM[:, 8:40].rearrange("p (a b) -> p a b", a=4)
    pG_b = k.buf()
    eC = [sbr(3, "eC%d" % i, [128, NJ, 512], BF16) for i in range(2)]
    eC_b = [b3() for _ in range(2)]
    pC = [ps("pC%d" % i, [128, 512], F32)[:, 0:321] for i in range(2)]
    pC_b = k.bufs(2, "pC")
    rsum = [sb("rsum%d" % i, [128, 1], F32) for i in range(4)]
    rsum_b = k.bufs(4, "rsum")
    ocmp = sb("ocmp", [128, 4, 2, 64], F32)
    ocmp_b = k.bufs(4, "ocmp")
    imp = sbr(3, "imp", [128, 4, 256], F32)
    imp_b = [b3() for _ in range(4)]
    m8a = sb("m8a", [128, 8], F32)
    m8b = sb("m8b", [128, 8], F32)
    m8_b = k.buf()
    imw = sb("imw", [128, 256], F32)
    imw_b = k.buf()
    selm = sb("selm", [128, 256], BF16)
    selm_b = k.buf()
    pTs = ps("pTs", [128, 2, 128], BF16)[:]
    pTs_b = k.buf()
    selT = [sbr(3, "selT%d" % i, [128, 2, 512], BF16) for i in range(2)]
    selT_b = [b3() for _ in range(2)]
    eS = [sbr(3, "eS%d" % i, [128, 512], BF16) for i in range(3)]
    eS_b = [b3() for _ in range(3)]
    pO = [ps("pOb%d" % i, [128, 512], F32)[:].rearrange("p (a b) -> p a b", a=4)[:, :, 0:65] for i in range(2)]
    pO_b = k.bufs(2, "pO")
    oacc = sb("oacc", [128, 4, 2, 64], F32)
    oacc_b = k.bufs(4, "oacc")
    rs2 = [sb("rs2_%d" % i, [128, 4, 1], F32) for i in range(2)]
    rs2_b = k.bufs(2, "rs2")
    osb = [sb("osb%d" % i, [128, 4, 128], BF16) for i in range(2)]
    osb_b = k.bufs(2, "osb")
    o_v = o.rearrange("(t p) f -> p t f", p=128)
    nes = [0]
    npo = [0]

    for G in range(NG):
        i = load_group(G, cnt)
        cnt += 1
        gi2 = G % 2
        r = G % 4
        nj = G // 4 + 1
        qs = slice(G * 512, (G + 1) * 512)
        for hh in range(2):
            pi = proj("qg", i, col0=128 * hh)
            k.op("act", lambda e, pi=pi, hh=hh, gi2=gi2: e.activation(out=QC[gi2][:, hh, :], in_=pS[pi][:], func=AF.Copy),
                 reads=[pS_b[pi]], writes=[QC_b[gi2]])
        rope_to(QR[gi2][:, 0, :], [QR_b[gi2][0]], "q1", "q1p", i, 0)
        rope_to(QR[gi2][:, 1, :], [QR_b[gi2][1]], "q2", "q2p", i, 1)
        for tt in range(4):
            for kk in range(8):
                k.op("pe", lambda e, kk=kk, tt=tt, i=i: e.matmul(
                    pG[:, tt, 0:6], xg[i][:, kk, tt * 128:(tt + 1) * 128], W["g"][:, kk, :], start=(kk == 0), stop=False),
                    reads=[W_b["g"], xg_b[i]], writes=[pG_b])
            k.op("pe", lambda e, tt=tt: e.matmul(pG[:, tt, 0:6], ones1[:], gbb[:], start=False, stop=True),
                 reads=[ones1_b, gbb_b], writes=[pG_b])
        k.op("act", lambda e, gi2=gi2: e.activation(out=GT[gi2][:], in_=pG[:, :, 0:6], func=AF.Sigmoid),
             reads=[pG_b], writes=[GT_b[gi2]])

        for h4 in range(4):
            ei = h4 % 2
            hp = slice(64 * (h4 % 2), 64 * (h4 % 2) + 64)
            for j in range(nj):
                pi = next_ps()
                msk = None
                if j == nj - 1:
                    msk = r
                elif j == nj - 2 and r == 0:
                    msk = 4
                k.op("pe", lambda e, pi=pi, hp=hp, j=j, gi2=gi2, h4=h4, msk=msk: e.matmul(
                    pS[pi][:], KCT[hp, 128 * j:128 * j + 128], QC[gi2][hp, h4 // 2, :], start=True, stop=(msk is None)),
                    reads=[KCT_b, QC_b[gi2]], writes=[pS_b[pi]])
                if msk is not None:
                    k.op("pe", lambda e, pi=pi, msk=msk: e.matmul(
                        pS[pi][:], c_sb["ident"][:], c_sb["cpneg"][:, msk, :], start=False, stop=True),
                        reads=[c_b["ident"], c_b["cpneg"]], writes=[pS_b[pi]])
                k.op("act", lambda e, pi=pi, ei=ei, j=j: e.activation(out=eC[ei][:, j, :], in_=pS[pi][:], func=AF.Exp),
                     reads=[pS_b[pi]], writes=[eC_b[ei]])
            own = (h4 // 2 == 0)
            for tc in range(4):
                ci = (h4 * 4 + tc) % 2
                for j in range(nj):
                    k.op("pe", lambda e, ci=ci, ei=ei, j=j, tc=tc, nj=nj: e.matmul(
                        pC[ci], eC[ei][:, j, tc * 128:(tc + 1) * 128], RC[:, j, :], start=(j == 0), stop=(j == nj - 1)),
                        reads=[eC_b[ei], RC_b], writes=[pC_b[ci]])
                ri = (h4 * 4 + tc) % 4
                k.op("dve", lambda e, ci=ci, ri=ri: e.tensor_scalar(rsum[ri][:], pC[ci][:, 64:65], 1e-30, None, ALU.add),
                     reads=[pC_b[ci]], writes=[rsum_b[ri]])
                k.op("dve", lambda e, ri=ri: e.reciprocal(rsum[ri][:], rsum[ri][:]),
                     reads=[rsum_b[ri]], writes=[rsum_b[ri]])
                if h4 == 0:
                    k.op("dve", lambda e, ci=ci, ri=ri, tc=tc: e.tensor_scalar(
                        imp[:, tc, :], pC[ci][:, 65:321], rsum[ri][:, 0:1], None, ALU.mult),
                        reads=[pC_b[ci], rsum_b[ri]], writes=[imp_b[tc]])
                else:
                    k.op("dve", lambda e, ci=ci, ri=ri, tc=tc: e.scalar_tensor_tensor(
                        imp[:, tc, :], pC[ci][:, 65:321], rsum[ri][:, 0:1], imp[:, tc, :], ALU.mult, ALU.add),
                        reads=[pC_b[ci], rsum_b[ri], imp_b[tc]], writes=[imp_b[tc]])
                if own:
                    k.op("dve", lambda e, ri=ri, gi2=gi2, tc=tc, h4=h4: e.tensor_tensor(
                        rsum[ri][:], rsum[ri][:], GT[gi2][:, tc, h4:h4 + 1], ALU.mult),
                        reads=[rsum_b[ri], GT_b[gi2]], writes=[rsum_b[ri]])
                    k.op("dve", lambda e, ci=ci, ri=ri, tc=tc, h4=h4: e.tensor_scalar(
                        ocmp[:, tc, h4, :], pC[ci][:, 0:64], rsum[ri][:, 0:1], None, ALU.mult),
                        reads=[pC_b[ci], rsum_b[ri]], writes=[ocmp_b[tc]])

        si = G % 2
        for tc in range(4):
            Wd = 8 * G + 2 * tc + 2
            if Wd <= 16:
                k.op("pool", lambda e, si=si, tc=tc: e.memset(selT[si][:, :, tc * 128:(tc + 1) * 128], 0.0),
                     writes=[selT_b[si]])
                continue
            k.op("dve", lambda e, tc=tc, Wd=Wd: e.tensor_copy(imw[:, 0:Wd], imp[:, tc, 0:Wd]),
                 reads=[imp_b[tc]], writes=[imw_b])
            k.op("dve", lambda e: e.memset(imw[:, 0:1], 1e6), writes=[imw_b])
            k.op("dve", lambda e, Wd=Wd: e.memset(imw[:, Wd - 2:Wd - 1], 2e6), writes=[imw_b])
            k.op("dve", lambda e, Wd=Wd: e.tensor_copy(imw[:, Wd - 1:Wd], fc_sb[:, 1:2]), reads=[fc_b], writes=[imw_b])
            k.op("dve", lambda e, Wd=Wd: e.tensor_tensor(imw[:, Wd - 3:Wd - 2], imw[:, Wd - 3:Wd - 2], fc_sb[:, 0:1], ALU.max),
                 reads=[fc_b, imw_b], writes=[imw_b])
            k.op("dve", lambda e, Wd=Wd: e.max(out=m8a[:], in_=imw[:, 0:Wd]), reads=[imw_b], writes=[m8_b])
            k.op("dve", lambda e, tc=tc, Wd=Wd: e.match_replace(
                out=imp[:, tc, 0:Wd], in_to_replace=m8a[:], in_values=imw[:, 0:Wd], imm_value=-2.0),
                reads=[imw_b, m8_b], writes=[imp_b[tc]])
            k.op("dve", lambda e, tc=tc, Wd=Wd: e.max(out=m8b[:], in_=imp[:, tc, 0:Wd]), reads=[imp_b[tc]], writes=[m8_b])
            k.op("dve", lambda e: e.memset(selm[:], 0.0), writes=[selm_b])
            k.op("dve", lambda e, Wd=Wd: e.tensor_scalar(
                selm[:, 0:Wd], imw[:, 0:Wd], m8b[:, 7:8], 1.0, ALU.is_ge, ALU.subtract),
                reads=[imw_b, m8_b], writes=[selm_b])
            for mh in range(2):
                k.op("pe", lambda e, mh=mh: e.transpose(pTs[:, mh, :], selm[:, mh * 128:(mh + 1) * 128], c_sb["ident"][:]),
                     reads=[selm_b, c_b["ident"]], writes=[pTs_b])
            k.op("act", lambda e, si=si, tc=tc: e.activation(out=selT[si][:, :, tc * 128:(tc + 1) * 128], in_=pTs, func=AF.Copy),
                 reads=[pTs_b], writes=[selT_b[si]])

        for hi in range(2):
            for br in range(2):
                if br == 0:
                    kbs = list(range(0, 4 * G + 4))
                    qv = QR[gi2][0:64, hi, :]
                    qb = QR_b[gi2][hi]
                    kp = slice(0, 64)
                    vc = slice(0, 65)
                else:
                    kbs = list(range(max(0, 4 * G - 4), 4 * G + 4))
                    qv = QR[gi2][64:128, 1 - hi, :]
                    qb = QR_b[gi2][1 - hi]
                    kp = slice(64, 128)
                    vc = slice(66, 131)
                po = npo[0] % 2
                npo[0] += 1
                for step, kb in enumerate(kbs):
                    pi = next_ps()
                    jd = kb - 4 * G
                    extra = []
                    if br == 0:
                        extra.append(("sel", kb))
                    if jd >= 0:
                        extra.append(("cm", jd))
                    if br == 1 and kb - (4 * G - 4) < 4 and 4 * G - 4 >= 0:
                        extra.append(("wm", kb - (4 * G - 4)))
                    k.op("pe", lambda e, pi=pi, kp=kp, kb=kb, qv=qv, extra=extra: e.matmul(
                        pS[pi][:], KSW[kp, 128 * kb:128 * kb + 128], qv, start=True, stop=(len(extra) == 0)),
                        reads=[KSW_b[kb // 4], qb], writes=[pS_b[pi]])
                    for xi, (kind, a) in enumerate(extra):
                        last = (xi == len(extra) - 1)
                        if kind == "sel":
                            k.op("pe", lambda e, pi=pi, a=a, si=si, last=last: e.matmul(
                                pS[pi][:], c_sb["eall"][:, (a % 64) * 128:(a % 64) * 128 + 128], selT[si][:, a // 64, :],
                                start=False, stop=last),
                                reads=[c_b["eall"], selT_b[si]], writes=[pS_b[pi]])
                        else:
                            cname = "cmneg" if kind == "cm" else "wmneg"
                            k.op("pe", lambda e, pi=pi, a=a, cname=cname, last=last: e.matmul(
                                pS[pi][:], c_sb["ident"][:], c_sb[cname][:, a, :], start=False, stop=last),
                                reads=[c_b["ident"], c_b[cname]], writes=[pS_b[pi]])
                    es = nes[0] % 3
                    nes[0] += 1
                    k.op("act", lambda e, pi=pi, es=es: e.activation(out=eS[es][:], in_=pS[pi][:], func=AF.Exp),
                         reads=[pS_b[pi]], writes=[eS_b[es]])
                    if dbg and G == 0 and hi == 0 and br == 0 and kb < 2:
                        dE = sb("dE%d" % kb, [128, 512], F32)
                        dE_b = k.buf()
                        k.op("dve", lambda e, dE=dE, es=es: e.tensor_copy(dE[:], eS[es][:]), reads=[eS_b[es]], writes=[dE_b])
                        dEo = nc.dram_tensor("dEo%d" % kb, [128, 512], F32, kind="ExternalOutput").ap()
                        k.dma("sp", dEo, dE[:], reads=[dE_b])
                        dS = sb("dS%d" % kb, [128, 512], F32)
                        dS_b = k.buf()
                        k.op("dve", lambda e, dS=dS, pi=pi: e.tensor_copy(dS[:], pS[pi][:]), reads=[pS_b[pi]], writes=[dS_b])
                        dSo = nc.dram_tensor("dSo%d" % kb, [128, 512], F32, kind="ExternalOutput").ap()
                        k.dma("sp", dSo, dS[:], reads=[dS_b])
                        k.wait("sp", [dE_b, dS_b])
                    for tc in range(4):
                        k.op("pe", lambda e, po=po, es=es, tc=tc, kb=kb, vc=vc, step=step, nk=len(kbs): e.matmul(
                            pO[po][:, tc, :], eS[es][:, tc * 128:(tc + 1) * 128], VSW[:, kb, vc],
                            start=(step == 0 and tc == 0), stop=(step == nk - 1)),
                            reads=[eS_b[es], VSW_b[kb // 4]], writes=[pO_b[po]])
                if dbg and G == 0 and hi == 0 and br == 0:
                    dP = sb("dP", [128, 4, 65], F32)
                    dP_b = k.buf()
                    k.op("dve", lambda e, po=po: e.tensor_copy(dP[:], pO[po]), reads=[pO_b[po]], writes=[dP_b])
                    dPo = nc.dram_tensor("dPo", [128, 260], F32, kind="ExternalOutput").ap()
                    k.dma("sp", dPo, dP[:].rearrange("p a b -> p (a b)"), reads=[dP_b])
                    dV = sb("dV", [128, 4, 132], F32)
                    dV_b = k.buf()
                    k.op("dve", lambda e: e.tensor_copy(dV[:], VSW[:, 0:4, :]), reads=[VSW_b[0]], writes=[dV_b])
                    dVo = nc.dram_tensor("dVo", [128, 528], F32, kind="ExternalOutput").ap()
                    k.dma("sp", dVo, dV[:].rearrange("p a b -> p (a b)"), reads=[dV_b])
                    k.wait("sp", [dP_b, dV_b])
                ri = po
                gcol = 2 * (br + 1) + hi
                k.op("dve", lambda e, po=po, ri=ri: e.reciprocal(rs2[ri][:], pO[po][:, :, 64:65]),
                     reads=[pO_b[po]], writes=[rs2_b[ri]])
                k.op("dve", lambda e, ri=ri, gi2=gi2, gcol=gcol: e.tensor_tensor(
                    rs2[ri][:], rs2[ri][:], GT[gi2][:, :, gcol:gcol + 1], ALU.mult),
                    reads=[rs2_b[ri], GT_b[gi2]], writes=[rs2_b[ri]])
                for tc in range(4):
                    prev = ocmp[:, tc, hi, :] if br == 0 else oacc[:, tc, hi, :]
                    prev_b = ocmp_b[tc] if br == 0 else oacc_b[tc]
                    k.op("dve", lambda e, po=po, ri=ri, tc=tc, hi=hi, prev=prev: e.scalar_tensor_tensor(
                        oacc[:, tc, hi, :], pO[po][:, tc, 0:64], rs2[ri][:, tc, :], prev, ALU.mult, ALU.add),
                        reads=[pO_b[po], rs2_b[ri], prev_b, oacc_b[tc]], writes=[oacc_b[tc]])
        if dbg and G == 0:
            dbg_t = nc.dram_tensor("dbg", [128, 2048], F32, kind="ExternalOutput").ap()
            dbuf = k.buf("dbg")
            k.dma("sp", dbg_t[:, 0:512], ocmp[:].rearrange("p a h d -> p (a h d)"), reads=ocmp_b, sembuf=dbuf)
            k.dma("sp", dbg_t[:, 512:1024], oacc[:].rearrange("p a h d -> p (a h d)"), reads=oacc_b, sembuf=dbuf)
            k.dma("sp", dbg_t[:, 1024:1048], GT[0][:].rearrange("p a g -> p (a g)"), reads=[GT_b[0]], sembuf=dbuf)
            k.dma("sp", dbg_t[:, 1048:1052], rs2[0][:].rearrange("p a g -> p (a g)"), reads=[rs2_b[0]], sembuf=dbuf)
            k.dma("sp", dbg_t[:, 1052:1056], rs2[1][:].rearrange("p a g -> p (a g)"), reads=[rs2_b[1]], sembuf=dbuf)
            k.wait("sp", [dbuf])
        oi = G % 2
        k.op("pool", lambda e, oi=oi: e.tensor_copy(osb[oi][:].rearrange("p a (h d) -> p a h d", h=2), oacc[:]),
             reads=oacc_b, writes=[osb_b[oi]])
        k.dma("sp", o_v[:, 4 * G:4 * G + 4, :], osb[oi][:], reads=[osb_b[oi]])
    k.wait("sp", osb_b)
    assert nc.sbuf_base <= reg_base, (nc.sbuf_base, reg_base)
    k.emit()
    return nc


def _rope_perm(wb):
    p = np.zeros_like(wb)
    p[:, 0:8] = wb[:, 8:16]
    p[:, 8:16] = wb[:, 0:8]
    return p


def l3_inputs(c, xT, nsa_w_kv, nsa_w_q, nsa_gate_b, kv_norm, g10, cmp_pos, cmp_w1, cmp_w2, cst):
    gq_ = c // 2
    hA, hB = 2 * c, 2 * c + 1
    partner = [h for h in range(4 * gq_, 4 * gq_ + 4) if h not in (hA, hB)]

    def kvc(br, kv):
        c0 = ((br * 2 + kv) * 4 + gq_) * 64
        return nsa_w_kv[:, c0:c0 + 64]

    def qc(h):
        return nsa_w_q[:, h * 64:(h + 1) * 64]

    cat = lambda *a: np.ascontiguousarray(np.concatenate(a, axis=1))
    gcols = [1024 + cc * 16 + h for cc in range(3) for h in (hA, hB)]
    m = {
        "xT": xT,
        "w_kvr": cat(kvc(0, 0), kvc(0, 1)),
        "w_ksw": cat(kvc(1, 0), kvc(2, 0)),
        "w_kswp": cat(_rope_perm(kvc(1, 0)), _rope_perm(kvc(2, 0))),
        "w_vsw": cat(kvc(1, 1), kvc(2, 1)),
        "w_qg": cat(qc(hA), qc(hB), qc(partner[0]), qc(partner[1])),
        "w_q1": cat(qc(hA), qc(hB)),
        "w_q1p": cat(_rope_perm(qc(hA)), _rope_perm(qc(hB))),
        "w_q2": cat(qc(hB), qc(hA)),
        "w_q2p": cat(_rope_perm(qc(hB)), _rope_perm(qc(hA))),
        "w_g": np.ascontiguousarray(nsa_w_q[:, gcols]),
        "gb": np.ascontiguousarray(nsa_gate_b[[cc * 16 + h for cc in range(3) for h in (hA, hB)]][None, :]),
        "gkv": _g_layout(kv_norm),
        "gq": _g_layout(g10),
        "cw1": np.ascontiguousarray(cmp_w1.reshape(2, 32, 64, 256).transpose(0, 2, 1, 3).reshape(128, 32, 256)),
        "cw2": np.ascontiguousarray(cmp_w2.reshape(2, 2, 128, 64).transpose(2, 1, 0, 3)),
        "cposT": np.ascontiguousarray(cmp_pos.transpose(0, 2, 1).reshape(128, 32)),
    }
    m.update(cst)
    return m


def run_L3(xT, nsa_w_kv, nsa_w_q, nsa_gate_b, kv_norm, g10, cmp_pos, cmp_w1, cmp_w2, ntok=S):
    nc = build_L3(ntok)
    cst = _consts_nsa(ntok)
    in_maps = [l3_inputs(c, xT, nsa_w_kv, nsa_w_q, nsa_gate_b, kv_norm, g10, cmp_pos, cmp_w1, cmp_w2, cst)
               for c in range(NCORE)]
    res = run_bass_kernel_spmd(nc, in_maps, core_ids=list(range(NCORE)))
    return np.concatenate([res.results[c]["o"] for c in range(NCORE)], axis=1)


def kernel(**inp):
    import time
    t0 = time.time()
    x = np.ascontiguousarray(np.asarray(inp["x"], dtype=np.float32)[0])
    g = np.asarray(inp["norm_gain"], dtype=np.float32)
    f = lambda name: np.asarray(inp[name], dtype=np.float32)
    o1 = run_L1(x, f("sb_w_qkv")[0], g[0, 0])
    print("[kernel] L1 done", time.time() - t0, flush=True)
    h1, xT = run_tok(False, x, o1, f("sb_w_o")[0], f("mlp_w1")[0], f("mlp_w2")[0], g[0, 1])
    print("[kernel] L2 done", time.time() - t0, flush=True)
    o3 = run_L3(xT, f("nsa_w_kv"), f("nsa_w_q")[0], f("nsa_gate_b")[0], f("kv_norm"), g[1, 0],
                f("cmp_pos"), f("cmp_w1"), f("cmp_w2"))
    print("[kernel] L3 done", time.time() - t0, flush=True)
    y = run_tok(True, h1, o3, f("nsa_w_o")[0], f("mlp_w1")[1], f("mlp_w2")[1], g[1, 1], f("final_norm"))
    print("[kernel] L4 done", time.time() - t0, flush=True)
    return np.ascontiguousarray(y[None].astype(np.float32))
```

```python
import numpy as np
import ml_dtypes
import concourse.bass as bass
import concourse.mybir as mybir
from concourse.bass_utils import run_bass_kernel_spmd

F32 = mybir.dt.float32
BF16 = mybir.dt.bfloat16
AF = mybir.ActivationFunctionType
ALU = mybir.AluOpType
NPBF = ml_dtypes.bfloat16

S = 16384
D = 1024
NCORE = 8
EPS = 1e-5
MASKV = 30000.0
ATTACH_WAITS = True


class Buf:
    __slots__ = ("w", "r", "dsem", "dcnt", "name")

    def __init__(self, name=""):
        self.w = None
        self.r = {}
        self.dsem = None
        self.dcnt = 0
        self.name = name


class K:
    ENGS = ("pe", "act", "dve", "pool", "sp")

    def __init__(self, nc):
        self.nc = nc
        self.ops = {e: [] for e in self.ENGS}
        self.esem = {e: nc.alloc_semaphore("es_" + e) for e in self.ENGS}
        self.nsem = 0
        self.dma_latest = {}

    def buf(self, name=""):
        return Buf(name)

    def bufs(self, n, name=""):
        return [Buf(name + str(i)) for i in range(n)]

    @staticmethod
    def _key(tok):
        return (tok[0], tok[1])

    def _deps(self, reads, writes):
        deps = {}

        def add(tok):
            k = self._key(tok)
            if k not in deps or deps[k][2] < tok[2]:
                deps[k] = tok

        for b in reads:
            if b.w is not None:
                add(b.w)
        for b in writes:
            if b.w is not None:
                add(b.w)
            for t in b.r.values():
                add(t)
        return list(deps.values())

    def _commit(self, tok, reads, writes):
        k = self._key(tok)
        for b in reads:
            b.r[k] = tok
        for b in writes:
            b.w = tok
            b.r = {}

    def op(self, eng, fn, reads=(), writes=()):
        deps = self._deps(reads, writes)
        idx = len(self.ops[eng])
        tok = ("c", eng, idx)
        self.ops[eng].append(["c", fn, deps, None])
        self._commit(tok, reads, writes)
        return tok

    def dma(self, eng, out, in_, reads=(), writes=(), sembuf=None):
        b = sembuf if sembuf is not None else (writes[0] if writes else reads[0])
        if b.dsem is None:
            b.dsem = self.nc.alloc_semaphore("ds%d" % self.nsem)
            self.nsem += 1
        deps = self._deps(reads, writes)
        b.dcnt += 16
        tok = ("d", b.dsem, b.dcnt)
        self.dma_latest[id(b.dsem)] = (b.dsem, b.dcnt)
        self.ops[eng].append(["d", (out, in_), deps, (b.dsem, 16)])
        self._commit(tok, reads, writes)
        return tok

    def wait(self, eng, bufs):
        deps = self._deps((), bufs)
        self.ops[eng].append(["w", None, deps, None])

    def barrier(self):
        deps = []
        for e in self.ENGS:
            for idx in range(len(self.ops[e]) - 1, -1, -1):
                if self.ops[e][idx][0] == "c":
                    deps.append(("c", e, idx))
                    break
        for sem, val in self.dma_latest.values():
            deps.append(("d", sem, val))
        for e in self.ENGS:
            self.ops[e].append(["w", None, list(deps), None])

    def collective(self, in_ap, out_ap, after, out_buf):
        sem = self.nc.alloc_semaphore("cs%d" % self.nsem)
        self.nsem += 1
        deps = self._deps((), list(after) + [out_buf])
        tok = ("d", sem, 1)
        self.ops["pool"].append(["x", (in_ap, out_ap), deps, (sem, 1)])
        self.dma_latest[id(sem)] = (sem, 1)
        self._commit(tok, (), [out_buf])
        return tok

    def emit(self):
        nc = self.nc
        needed = {e: set() for e in self.ENGS}
        for e in self.ENGS:
            for o in self.ops[e]:
                for t in o[2]:
                    if t[0] == "c":
                        if t[1] == "pe" and e == "pe":
                            continue
                        needed[t[1]].add(t[2])
        rank = {}
        for e in self.ENGS:
            for i, idx in enumerate(sorted(needed[e])):
                rank[(e, idx)] = i + 1
        engobj = {"pe": nc.tensor, "act": nc.scalar, "dve": nc.vector, "pool": nc.gpsimd, "sp": nc.sync}
        with nc.Block() as block:
            deco = {"pe": block.tensor, "act": block.scalar, "dve": block.vector,
                    "pool": block.gpsimd, "sp": block.sync}

            def make(e):
                def body(eng):
                    waited = {}
                    for idx, o in enumerate(self.ops[e]):
                        kind, fn, deps, inc = o
                        todo = {}
                        for t in deps:
                            if t[0] == "c":
                                if t[1] == "pe" and e == "pe":
                                    continue
                                sem = self.esem[t[1]]
                                val = rank[(t[1], t[2])]
                                key = ("c", t[1])
                            else:
                                sem = t[1]
                                val = t[2]
                                key = ("d", id(t[1]))
                            if waited.get(key, 0) < val and (key not in todo or todo[key][1] < val):
                                todo[key] = (sem, val)
                        todo = list(todo.items())
                        attach = None
                        if kind in ("c", "d") and todo and ATTACH_WAITS:
                            attach = todo.pop()
                        for key, (sem, val) in todo:
                            eng.wait_ge(sem, val)
                            waited[key] = val
                        if kind == "c":
                            ins = fn(eng)
                            if attach is not None:
                                ins._wait_ge(attach[1][0], attach[1][1])
                                waited[attach[0]] = attach[1][1]
                            if (e, idx) in rank:
                                ins.then_inc(self.esem[e], 1)
                        elif kind == "d":
                            ins = eng.dma_start(out=fn[0], in_=fn[1])
                            if attach is not None:
                                ins._wait_ge(attach[1][0], attach[1][1])
                                waited[attach[0]] = attach[1][1]
                            ins.then_inc(inc[0], inc[1])
                        elif kind == "x":
                            eng.collective_compute("AllGather", ALU.bypass, replica_groups=[list(range(NCORE))],
                                                   ins=[fn[0].opt()], outs=[fn[1].opt()]).then_inc(inc[0])
                return body

            for e in self.ENGS:
                if self.ops[e]:
                    deco[e](make(e))


class PSAlloc:
    def __init__(self, nc):
        self.banks = [nc.alloc_psum_tensor("bank%d" % i, [128, 512], F32) for i in range(8)]
        self.i = 0

    def reset(self):
        self.i = 0

    def __call__(self, name, shape, dtype=F32):
        b = self.banks[self.i]
        self.i += 1
        assert self.i <= 8, "out of PSUM banks"
        ap = b[:] if dtype == F32 else b[:].bitcast(dtype)
        n = 1
        for d_ in shape[1:]:
            n *= d_
        flat = ap[:, 0:n]
        if len(shape) == 2:
            return flat
        if len(shape) == 3:
            return flat.rearrange("p (a b) -> p a b", a=shape[1])
        return flat.rearrange("p (a b c) -> p a b c", a=shape[1], b=shape[2])


class SBAlloc:
    def __init__(self, nc):
        self.nc = nc
        self.base = (nc.sbuf_base + 63) // 64 * 64
        self.top = nc.sbuf_top
        self.cur = self.base
        self.phase = 0

    def reset(self):
        self.cur = self.base
        self.phase += 1

    def __call__(self, name, shape, dtype):
        n = 1
        for d_ in shape[1:]:
            n *= d_
        nbytes = n * (4 if dtype == F32 else 2)
        off = self.cur
        self.cur = off + (nbytes + 63) // 64 * 64
        assert self.cur <= self.top, ("SBUF overflow", name, self.cur, self.top)
        return self.nc.alloc_sbuf_tensor_at("p%d_%s" % (self.phase, name), shape, dtype, offset=off)


def _bf(a):
    return np.ascontiguousarray(a.astype(NPBF))


def _consts_attn():
    c = {}
    j = np.arange(128)[:, None]
    s = np.arange(128)[None, :]
    c["uneg"] = _bf(np.where(j >= s, -1.0, 0.0))
    c["oneg"] = _bf(-np.ones((128, 128)))
    c["ident"] = _bf(np.eye(128))
    sl = np.arange(128)[:, None]
    tl = np.arange(512)[None, :]
    cm01 = np.stack([(sl + 128 * jj < tl) for jj in range(4)]).astype(np.float32)
    c["cm01"] = _bf(cm01.transpose(1, 0, 2))
    c["cmneg"] = _bf(((cm01 - 1.0) * MASKV).transpose(1, 0, 2))
    return c


def emit_L1(nc, k, sb, ps, T, ntok=S):
    NT = ntok // 128
    NG = ntok // 512
    x, wq, wk, wv, g, o = T["x"], T["wq"], T["wk"], T["wv"], T["g"], T["o"]
    cst = {n: T[n] for n in ("uneg", "oneg", "ident", "cm01", "cmneg")}
    c_sb = {n: sb("c_" + n, list(a.shape), BF16) for n, a in cst.items()}
    c_b = {n: k.buf(n) for n in cst}
    for n in cst:
        k.dma("sp", c_sb[n][:], cst[n], writes=[c_b[n]])
    g_sb = sb("g_sb", [128, 8], F32)
    g_b = k.buf("g")
    k.dma("sp", g_sb[:], g, writes=[g_b])
    wst = sb("wst", [128, 8, 128], F32)
    wst_b = k.buf("wst")
    wb = {}
    wb_b = {}
    for name, w_ap, scl in (("q", wq, 0.125), ("k", wk, 1.0), ("v", wv, 1.0)):
        wb[name] = sb("wb_" + name, [128, 8, 128], BF16)
        wb_b[name] = k.buf("wb" + name)
        k.dma("sp", wst[:], w_ap.rearrange("(k p) n -> p k n", p=128), writes=[wst_b])
        for kk in range(8):
            k.op("dve", lambda e, kk=kk, name=name, scl=scl: e.tensor_scalar(
                wb[name][:, kk, :], wst[:, kk, :], g_sb[:, kk:kk + 1], scl, ALU.mult, ALU.mult),
                reads=[wst_b, g_b], writes=[wb_b[name]])

    QT = sb("QT", [128, ntok], BF16)
    KT = sb("KT", [128, ntok], BF16)
    V = sb("V", [128, NT, 128], BF16)
    QT_b = k.bufs(NG, "QT")
    KT_b = k.bufs(NG, "KT")
    V_b = k.bufs(NG, "V")

    NXB = 3
    xin = [sb("xin%d" % i, [128, D], F32) for i in range(NXB)]
    xin_b = k.bufs(NXB, "xin")
    junk = sb("junk", [128, D], F32)
    junk_b = k.buf("junk")
    ssq = [sb("ssq%d" % i, [128, 1], F32) for i in range(2)]
    ssq_b = k.bufs(2, "ssq")
    rstd = [sb("rstd%d" % i, [128, 1], F32) for i in range(2)]
    rstd_b = k.bufs(2, "rstd")
    xh = [sb("xh%d" % i, [128, D], BF16) for i in range(2)]
    xh_b = k.bufs(2, "xh")
    xT = [sb("xT%d" % i, [128, 8, 512], BF16) for i in range(2)]
    xT_b = k.bufs(2, "xT")
    pT = [ps("pT%d" % i, [128, 8, 128], BF16) for i in range(2)]
    pT_b = k.bufs(2, "pT")
    pA = [ps("pA%d" % i, [128, 512], F32) for i in range(4)]
    pA_b = k.bufs(4, "pA")
    pO = [ps("pO%d" % i, [128, 512], F32) for i in range(2)]
    pO_b = k.bufs(2, "pO")

    nproj = 0
    for gi in range(NG):
        xt_i = gi % 2
        for ti in range(4):
            t = gi * 4 + ti
            xi = t % NXB
            si = t % 2
            k.dma("sp", xin[xi][:], x[t * 128:(t + 1) * 128, :], writes=[xin_b[xi]])
            k.op("act", lambda e, xi=xi, si=si: e.activation(
                out=junk[:], in_=xin[xi][:], func=AF.Square, accum_out=ssq[si][:]),
                reads=[xin_b[xi]], writes=[junk_b, ssq_b[si]])
            k.op("act", lambda e, si=si: e.activation(
                out=rstd[si][:], in_=ssq[si][:], func=AF.Sqrt, scale=1.0 / D, bias=EPS),
                reads=[ssq_b[si]], writes=[rstd_b[si]])
            k.op("dve", lambda e, si=si: e.reciprocal(rstd[si][:], rstd[si][:]),
                reads=[rstd_b[si]], writes=[rstd_b[si]])
            k.op("dve", lambda e, xi=xi, si=si: e.tensor_scalar(
                xh[si][:], xin[xi][:], rstd[si][:, 0:1], None, ALU.mult),
                reads=[xin_b[xi], rstd_b[si]], writes=[xh_b[si]])
            for kk in range(8):
                k.op("pe", lambda e, si=si, kk=kk: e.transpose(
                    pT[si][:, kk, :], xh[si][:, kk * 128:(kk + 1) * 128], c_sb["ident"][:]),
                    reads=[xh_b[si], c_b["ident"]], writes=[pT_b[si]])
            k.op("act", lambda e, si=si, xt_i=xt_i, ti=ti: e.activation(
                out=xT[xt_i][:, :, ti * 128:(ti + 1) * 128], in_=pT[si][:], func=AF.Copy),
                reads=[pT_b[si]], writes=[xT_b[xt_i]])
        for name, dst, dst_b in (("q", QT, QT_b), ("k", KT, KT_b)):
            pi = nproj % 2
            nproj += 1
            for kk in range(8):
                k.op("pe", lambda e, pi=pi, kk=kk, name=name, xt_i=xt_i: e.matmul(
                    pA[pi][:], wb[name][:, kk, :], xT[xt_i][:, kk, :], start=(kk == 0), stop=(kk == 7)),
                    reads=[wb_b[name], xT_b[xt_i]], writes=[pA_b[pi]])
            k.op("dve", lambda e, pi=pi, dst=dst, gi=gi: e.tensor_copy(
                dst[:, gi * 512:(gi + 1) * 512], pA[pi][:]),
                reads=[pA_b[pi]], writes=[dst_b[gi]])
        pi = nproj % 2
        nproj += 1
        for ti in range(4):
            for kk in range(8):
                k.op("pe", lambda e, pi=pi, kk=kk, ti=ti, xt_i=xt_i: e.matmul(
                    pA[pi][:, ti * 128:(ti + 1) * 128], xT[xt_i][:, kk, ti * 128:(ti + 1) * 128],
                    wb["v"][:, kk, :], start=(kk == 0), stop=(kk == 7)),
                    reads=[wb_b["v"], xT_b[xt_i]], writes=[pA_b[pi]])
        k.op("dve", lambda e, pi=pi, gi=gi: e.tensor_copy(
            V[:, gi * 4:(gi + 1) * 4, :], pA[pi][:].rearrange("p (a b) -> p a b", a=4)),
            reads=[pA_b[pi]], writes=[V_b[gi]])

    spb = [sb("spb%d" % i, [128, 512], BF16) for i in range(6)]
    spb_b = k.bufs(6, "spb")
    spsv = [sb("spsv%d" % i, [128, 512], BF16) for i in range(6)]
    spsv_b = k.bufs(6, "spsv")
    aT = [sb("aT%d" % i, [128, 512], BF16) for i in range(4)]
    aT_b = k.bufs(4, "aT")
    osb = [sb("osb%d" % i, [128, 4, 128], BF16) for i in range(2)]
    osb_b = k.bufs(2, "osb")
    oTs = [sb("oTs%d" % i, [64, 512], BF16) for i in range(2)]
    oTs_b = k.bufs(2, "oTs")
    o_v = o.rearrange("(t p) f -> p t f", p=128)
    tiles = []
    for G in range(NG):
        for h in range(2):
            nkb = 4 * (G + 1)
            for step, kb in enumerate(range(nkb - 1, -1, -1)):
                tiles.append((len(tiles), G, h, kb, step, nkb))

    def S1(tl):
        i, G, h, kb, step, nkb = tl
        a = i % 4
        sx = i % 6
        hp = slice(64 * h, 64 * h + 64)
        jd = kb - 4 * G
        k.op("pe", lambda e: e.matmul(pA[a][:], KT[hp, 128 * kb:128 * kb + 128], QT[hp, 512 * G:512 * G + 512],
                                      start=True, stop=True),
             reads=[KT_b[kb // 4], QT_b[G]], writes=[pA_b[a]])
        k.op("act", lambda e: e.activation(out=spb[sx][:], in_=pA[a][:], func=AF.Softplus),
             reads=[pA_b[a]], writes=[spb_b[sx]])
        if jd >= 0:
            k.op("dve", lambda e: e.tensor_tensor(spb[sx][:], spb[sx][:], c_sb["cm01"][:, jd, :], ALU.mult),
                 reads=[spb_b[sx], c_b["cm01"]], writes=[spb_b[sx]])
        if step < nkb - 1:
            v = i % 6
            if step == 0:
                k.op("dve", lambda e: e.tensor_copy(spsv[v][:], spb[sx][:]), reads=[spb_b[sx]], writes=[spsv_b[v]])
            else:
                pv = (i - 1) % 6
                k.op("dve", lambda e: e.tensor_tensor(spsv[v][:], spsv[pv][:], spb[sx][:], ALU.add),
                     reads=[spb_b[sx], spsv_b[pv]], writes=[spsv_b[v]])

    def S2(tl):
        i, G, h, kb, step, nkb = tl
        a = i % 4
        sx = i % 6
        jd = kb - 4 * G
        last_is_u = (step == 0 and jd < 0)
        k.op("pe", lambda e: e.matmul(pA[a][:], c_sb["uneg"][:], spb[sx][:], start=False, stop=last_is_u),
             reads=[c_b["uneg"], spb_b[sx]], writes=[pA_b[a]])
        if step > 0:
            pv = (i - 1) % 6
            k.op("pe", lambda e: e.matmul(pA[a][:], c_sb["oneg"][:], spsv[pv][:], start=False, stop=(jd < 0)),
                 reads=[c_b["oneg"], spsv_b[pv]], writes=[pA_b[a]])
        if jd >= 0:
            k.op("pe", lambda e: e.matmul(pA[a][:], c_sb["ident"][:], c_sb["cmneg"][:, jd, :], start=False, stop=True),
                 reads=[c_b["ident"], c_b["cmneg"]], writes=[pA_b[a]])
        k.op("act", lambda e: e.activation(out=aT[a][:], in_=pA[a][:], func=AF.Exp),
             reads=[pA_b[a]], writes=[aT_b[a]])

    def S3(tl):
        i, G, h, kb, step, nkb = tl
        a = i % 4
        acc_i = (G * 2 + h) % 2
        oi = G % 2
        k.op("pe", lambda e: e.matmul(
            pO[acc_i][0:64, :], V[:, kb, 64 * h:64 * h + 64], aT[a][:],
            start=(step == 0), stop=(step == nkb - 1)),
            reads=[aT_b[a], V_b[kb // 4]], writes=[pO_b[acc_i]])
        if step == nkb - 1:
            k.op("act", lambda e: e.activation(out=oTs[acc_i][:], in_=pO[acc_i][0:64, :], func=AF.Copy),
                 reads=[pO_b[acc_i]], writes=[oTs_b[acc_i]])
            for tc in range(4):
                k.op("pe", lambda e, tc=tc: e.transpose(
                    pT[acc_i][:, tc, 0:64], oTs[acc_i][:, tc * 128:(tc + 1) * 128], c_sb["ident"][0:64, 0:64]),
                    reads=[oTs_b[acc_i], c_b["ident"]], writes=[pT_b[acc_i]])
            k.op("dve", lambda e: e.tensor_copy(osb[oi][:, :, 64 * h:64 * h + 64], pT[acc_i][:, 0:4, 0:64]),
                 reads=[pT_b[acc_i]], writes=[osb_b[oi]])
            if h == 1:
                k.dma("sp", o_v[:, 4 * G:4 * G + 4, :], osb[oi][:], reads=[osb_b[oi]])

    NTL = len(tiles)
    NP = (NTL + 1) // 2
    for m in range(-2, NP):
        for t in (2 * (m + 2), 2 * (m + 2) + 1):
            if 0 <= t < NTL:
                S1(tiles[t])
        for t in (2 * (m + 1), 2 * (m + 1) + 1):
            if 0 <= t < NTL:
                S2(tiles[t])
        for t in (2 * m, 2 * m + 1):
            if 0 <= t < NTL:
                S3(tiles[t])
    return osb_b


L1_CONST_SHAPES = (("uneg", [128, 128]), ("oneg", [128, 128]), ("ident", [128, 128]),
                   ("cm01", [128, 4, 512]), ("cmneg", [128, 4, 512]))


def l1_tensors(nc, ntok, pre=""):
    di = lambda name, shp, dt=F32: nc.dram_tensor(pre + name, shp, dt, kind="ExternalInput").ap()
    T = {"x": di("x", [ntok, D]), "wq": di("wq", [D, 128]), "wk": di("wk", [D, 128]), "wv": di("wv", [D, 128]),
         "g": di("g", [128, 8])}
    for name, shp in L1_CONST_SHAPES:
        T[name] = di(name, shp, BF16)
    return T


def build_L1(ntok=S):
    nc = bass.Bass("TRN2", target_bir_lowering=False)
    T = l1_tensors(nc, ntok)
    T["o"] = nc.dram_tensor("o", [ntok, 128], BF16, kind="ExternalOutput").ap()
    k = K(nc)
    w = emit_L1(nc, k, SBAlloc(nc), PSAlloc(nc), T, ntok)
    k.wait("sp", w)
    k.emit()
    return nc


def run_L1(x2d, sb_w_qkv, g00, ntok=S):
    nc = build_L1(ntok)
    cst = _consts_attn()
    gl = np.ascontiguousarray(g00.reshape(8, 128).T)
    in_maps = []
    for c in range(NCORE):
        m = {"x": x2d, "g": gl}
        for i, nm in enumerate(("wq", "wk", "wv")):
            m[nm] = np.ascontiguousarray(sb_w_qkv[:, i * 1024 + 128 * c: i * 1024 + 128 * c + 128])
        m.update(cst)
        in_maps.append(m)
    res = run_bass_kernel_spmd(nc, in_maps, core_ids=list(range(NCORE)))
    return np.concatenate([res.results[c]["o"] for c in range(NCORE)], axis=1)


def emit_tok(nc, k, sb, ps, T, final, NTOK=2048):
    NT = NTOK // 128
    NTG = NTOK // 512
    hin, oall, selm, wo, w1, w2, g, ident = (T[n] for n in ("hin", "oall", "sel", "wo", "w1", "w2", "g", "ident"))
    pre_rd = [b for b in (T.get("oall_b"), T.get("hin_b")) if b is not None]
    if final:
        gfin, y = T["gfin"], T["y"]
    else:
        hout, xT_out = T["hout"], T["xT"]
    id_sb = sb("id_sb", [128, 128], BF16)
    id_b = k.buf("id")
    k.dma("sp", id_sb[:], ident, writes=[id_b])
    sel_sb = sb("sel_sb", [128, 8, 128], BF16)
    sel_b = k.buf("sel")
    k.dma("sp", sel_sb[:], selm, writes=[sel_b])
    g_sb = sb("g_sb", [128, 8], F32)
    g_b = k.buf("g")
    k.dma("sp", g_sb[:], g, writes=[g_b])
    if final:
        gf_sb = sb("gf_sb", [128, D], F32)
        gf_b = k.buf("gf")
        k.dma("sp", gf_sb[:], gfin, writes=[gf_b])

    h = sb("h", [128, NT, D], F32)
    h_b = k.bufs(NT, "h")
    actT = sb("actT", [128, 8, NTOK], BF16)
    actT_b = k.bufs(NT, "actT")
    wstA = sb("wstA", [128, 4, D], F32)
    wstA_b = k.buf("wstA")
    wstB = sb("wstB", [128, 8, 512], F32)
    wstB_b = k.buf("wstB")
    wo_sb = sb("wo_sb", [128, 8, D], BF16)
    wo_b = k.bufs(2, "wo")
    w1b = [sb("w1b%d" % i, [128, 8, 512], BF16) for i in range(2)]
    w1b_b = k.bufs(2, "w1b")
    w2b = [sb("w2b%d" % i, [128, 4, D], BF16) for i in range(2)]
    w2b_b = k.bufs(2, "w2b")
    h1T = [sb("h1T%d" % i, [128, 4, 512], BF16) for i in range(2)]
    h1T_b = [k.bufs(4, "h1T%d_" % i) for i in range(2)]
    rl = [sb("rl%d" % i, [128, 512], F32) for i in range(2)]
    rl_b = k.bufs(2, "rl")
    junk = sb("junk", [128, D], BF16)
    junk_b = k.buf("junk")
    ssq = [sb("ssq%d" % i, [128, 1], F32) for i in range(2)]
    ssq_b = k.bufs(2, "ssq")
    rstd = [sb("rstd%d" % i, [128, 1], F32) for i in range(2)]
    rstd_b = k.bufs(2, "rstd")
    xh = [sb("xh%d" % i, [128, D], BF16) for i in range(2)]
    xh_b = k.bufs(2, "xh")
    pT = [ps("pT%d" % i, [128, 8, 128], BF16) for i in range(2)]
    pT_b = k.bufs(2, "pT")
    pA = [ps("pA%d" % i, [128, 512], F32) for i in range(4)]
    pA_b = k.bufs(4, "pA")
    npa = [0]

    def next_pa():
        i = npa[0] % 4
        npa[0] += 1
        return i

    hin_v = hin.rearrange("(t p) f -> p t f", p=128)
    for t in range(NT):
        k.dma("sp", h[:, t, :], hin_v[:, t, :], reads=pre_rd, writes=[h_b[t]], sembuf=h_b[t])
    cand = wstB[:].bitcast(BF16).rearrange("p k (a f) -> p (k a) f", f=128)
    oall_v = oall.rearrange("(r c t p) f -> t p (r c) f", r=8, c=8, t=NT, p=128)
    for t in range(NT):
        k.dma("sp", cand, oall_v[t], reads=pre_rd, writes=[wstB_b], sembuf=wstB_b)
        for rh in range(2):
            pi = next_pa()
            for r4 in range(4):
                r = rh * 4 + r4
                for c in range(8):
                    k.op("pe", lambda e, pi=pi, r=r, r4=r4, c=c: e.matmul(
                        pA[pi][:, r4 * 128:(r4 + 1) * 128], cand[:, r * 8 + c, :], sel_sb[:, c, :],
                        start=(c == 0), stop=(c == 7)),
                        reads=[wstB_b, sel_b], writes=[pA_b[pi]])
            k.op("act", lambda e, pi=pi, rh=rh, t=t: e.activation(
                out=actT[:, rh * 4:(rh + 1) * 4, t * 128:(t + 1) * 128],
                in_=pA[pi][:].rearrange("p (a b) -> p a b", a=4), func=AF.Copy),
                reads=[pA_b[pi]], writes=[actT_b[t]])
    wo_v = wo.rearrange("(k p) n -> p k n", p=128)
    for half in range(2):
        k.dma("sp", wstA[:], wo_v[:, half * 4:(half + 1) * 4, :], writes=[wstA_b])
        k.op("dve", lambda e, half=half: e.tensor_copy(wo_sb[:, half * 4:(half + 1) * 4, :], wstA[:]),
             reads=[wstA_b], writes=[wo_b[half]])

    for t in range(NT):
        for nh in range(2):
            pi = next_pa()
            for kk in range(8):
                k.op("pe", lambda e, pi=pi, kk=kk, t=t, nh=nh: e.matmul(
                    pA[pi][:], actT[:, kk, t * 128:(t + 1) * 128], wo_sb[:, kk, nh * 512:(nh + 1) * 512],
                    start=(kk == 0), stop=(kk == 7)),
                    reads=[actT_b[t], wo_b[kk // 4]], writes=[pA_b[pi]])
            k.op("dve", lambda e, pi=pi, t=t, nh=nh: e.tensor_tensor(
                h[:, t, nh * 512:(nh + 1) * 512], h[:, t, nh * 512:(nh + 1) * 512], pA[pi][:], ALU.add),
                reads=[pA_b[pi], h_b[t]], writes=[h_b[t]])

    def norm_T(t):
        si = t % 2
        k.op("act", lambda e, si=si, t=t: e.activation(
            out=junk[:], in_=h[:, t, :], func=AF.Square, accum_out=ssq[si][:]),
            reads=[h_b[t]], writes=[junk_b, ssq_b[si]])
        k.op("act", lambda e, si=si: e.activation(
            out=rstd[si][:], in_=ssq[si][:], func=AF.Sqrt, scale=1.0 / D, bias=EPS),
            reads=[ssq_b[si]], writes=[rstd_b[si]])
        k.op("dve", lambda e, si=si: e.reciprocal(rstd[si][:], rstd[si][:]),
             reads=[rstd_b[si]], writes=[rstd_b[si]])
        k.op("dve", lambda e, si=si, t=t: e.tensor_scalar(
            xh[si][:], h[:, t, :], rstd[si][:, 0:1], None, ALU.mult),
            reads=[h_b[t], rstd_b[si]], writes=[xh_b[si]])
        for kk in range(8):
            k.op("pe", lambda e, si=si, kk=kk: e.transpose(
                pT[si][:, kk, :], xh[si][:, kk * 128:(kk + 1) * 128], id_sb[:]),
                reads=[xh_b[si], id_b], writes=[pT_b[si]])
        k.op("act", lambda e, si=si, t=t: e.activation(
            out=actT[:, :, t * 128:(t + 1) * 128], in_=pT[si][:], func=AF.Copy),
            reads=[pT_b[si]], writes=[actT_b[t]])

    for t in range(NT):
        norm_T(t)

    w1_v = w1.rearrange("(k p) f -> p k f", p=128)
    w2_v = w2.rearrange("(c p) n -> p c n", p=128)
    for fg in range(8):
        wi = fg % 2
        k.dma("sp", wstB[:], w1_v[:, :, fg * 512:(fg + 1) * 512], writes=[wstB_b])
        for kk in range(8):
            k.op("pool", lambda e, wi=wi, kk=kk: e.tensor_scalar(
                w1b[wi][:, kk, :], wstB[:, kk, :], g_sb[:, kk:kk + 1], None, ALU.mult),
                reads=[wstB_b, g_b], writes=[w1b_b[wi]])
        k.dma("sp", wstA[:], w2_v[:, fg * 4:(fg + 1) * 4, :], writes=[wstA_b])
        k.op("pool", lambda e, wi=wi: e.tensor_copy(w2b[wi][:], wstA[:]),
             reads=[wstA_b], writes=[w2b_b[wi]])
        for tg in range(NTG):
            hi = (fg * NTG + tg) % 2
            for fc in range(4):
                pi = next_pa()
                for kk in range(8):
                    k.op("pe", lambda e, pi=pi, kk=kk, fc=fc, tg=tg, wi=wi: e.matmul(
                        pA[pi][:], w1b[wi][:, kk, fc * 128:(fc + 1) * 128],
                        actT[:, kk, tg * 512:(tg + 1) * 512], start=(kk == 0), stop=(kk == 7)),
                        reads=[w1b_b[wi]] + actT_b[tg * 4:(tg + 1) * 4], writes=[pA_b[pi]])
                ri = fc % 2
                k.op("act", lambda e, pi=pi, ri=ri: e.activation(out=rl[ri][:], in_=pA[pi][:], func=AF.Relu),
                     reads=[pA_b[pi]], writes=[rl_b[ri]])
                k.op("dve", lambda e, ri=ri, hi=hi, fc=fc: e.tensor_tensor(
                    h1T[hi][:, fc, :], rl[ri][:], rl[ri][:], ALU.mult),
                    reads=[rl_b[ri]], writes=[h1T_b[hi][fc]])
            for tt in range(4):
                t = tg * 4 + tt
                for nh in range(2):
                    pi = next_pa()
                    for fc in range(4):
                        k.op("pe", lambda e, pi=pi, fc=fc, tt=tt, nh=nh, hi=hi, wi=wi: e.matmul(
                            pA[pi][:], h1T[hi][:, fc, tt * 128:(tt + 1) * 128],
                            w2b[wi][:, fc, nh * 512:(nh + 1) * 512], start=(fc == 0), stop=(fc == 3)),
                            reads=[h1T_b[hi][fc], w2b_b[wi]], writes=[pA_b[pi]])
                    k.op("dve", lambda e, pi=pi, t=t, nh=nh: e.tensor_tensor(
                        h[:, t, nh * 512:(nh + 1) * 512], h[:, t, nh * 512:(nh + 1) * 512], pA[pi][:], ALU.add),
                        reads=[pA_b[pi], h_b[t]], writes=[h_b[t]])

    if final:
        y_v = y.rearrange("(t p) f -> p t f", p=128)
        for t in range(NT):
            si = t % 2
            k.op("act", lambda e, si=si, t=t: e.activation(
                out=junk[:], in_=h[:, t, :], func=AF.Square, accum_out=ssq[si][:]),
                reads=[h_b[t]], writes=[junk_b, ssq_b[si]])
            k.op("act", lambda e, si=si: e.activation(
                out=rstd[si][:], in_=ssq[si][:], func=AF.Sqrt, scale=1.0 / D, bias=EPS),
                reads=[ssq_b[si]], writes=[rstd_b[si]])
            k.op("dve", lambda e, si=si: e.reciprocal(rstd[si][:], rstd[si][:]),
                 reads=[rstd_b[si]], writes=[rstd_b[si]])
            k.op("dve", lambda e, si=si, t=t: e.scalar_tensor_tensor(
                h[:, t, :], h[:, t, :], rstd[si][:, 0:1], gf_sb[:], ALU.mult, ALU.mult),
                reads=[h_b[t], rstd_b[si], gf_b], writes=[h_b[t]])
            k.dma("sp", y_v[:, t, :], h[:, t, :], reads=[h_b[t]])
        ret = h_b
    else:
        ho_v = hout.rearrange("(t p) f -> p t f", p=128)
        xTo_v = xT_out.rearrange("(k p) t -> p k t", p=128)
        for t in range(NT):
            k.dma("sp", ho_v[:, t, :], h[:, t, :], reads=[h_b[t]])
            norm_T(t)
        for tg in range(NTG):
            k.dma("sp", xTo_v[:, :, tg * 512:(tg + 1) * 512], actT[:, :, tg * 512:(tg + 1) * 512],
                  reads=actT_b[tg * 4:(tg + 1) * 4], sembuf=actT_b[tg * 4])
        ret = h_b + actT_b
    return ret


def tok_tensors(nc, final, pre="", NTOK=2048):
    di = lambda name, shp, dt=F32: nc.dram_tensor(pre + name, shp, dt, kind="ExternalInput").ap()
    T = {"sel": di("sel", [128, 8, 128], BF16), "wo": di("wo", [D, D]), "w1": di("w1", [D, 4 * D]),
         "w2": di("w2", [4 * D, D]), "g": di("g", [128, 8]), "ident": di("ident", [128, 128], BF16)}
    if final:
        T["gfin"] = di("gfin", [128, D])
    return T


def build_tok(final, NTOK=2048):
    nc = bass.Bass("TRN2", target_bir_lowering=False)
    T = tok_tensors(nc, final)
    T["hin"] = nc.dram_tensor("hin", [NTOK, D], F32, kind="ExternalInput").ap()
    T["oall"] = nc.dram_tensor("oall", [8 * S, 128], BF16, kind="ExternalInput").ap()
    if final:
        T["y"] = nc.dram_tensor("y", [NTOK, D], F32, kind="ExternalOutput").ap()
    else:
        T["hout"] = nc.dram_tensor("hout", [NTOK, D], F32, kind="ExternalOutput").ap()
        T["xT"] = nc.dram_tensor("xT", [D, NTOK], BF16, kind="ExternalOutput").ap()
    k = K(nc)
    w = emit_tok(nc, k, SBAlloc(nc), PSAlloc(nc), T, final)
    k.wait("sp", w)
    k.emit()
    return nc


def _g_layout(gv):
    return np.ascontiguousarray(gv.reshape(8, 128).T)


def _sel_const(c):
    sel = np.zeros((128, 8, 128), np.float32)
    sel[:, c, :] = np.eye(128, dtype=np.float32)
    return _bf(sel)


def run_tok(final, hin_full, o_full, wo, w1, w2, g_mlp, gfin=None):
    nc = build_tok(final)
    ident = _bf(np.eye(128))
    oall = np.ascontiguousarray(np.concatenate([o_full[:, 128 * r:128 * (r + 1)] for r in range(8)], axis=0))
    in_maps = []
    for c in range(NCORE):
        sl = slice(2048 * c, 2048 * (c + 1))
        m = {"hin": np.ascontiguousarray(hin_full[sl]), "oall": oall, "sel": _sel_const(c),
             "wo": wo, "w1": w1, "w2": w2, "g": _g_layout(g_mlp), "ident": ident}
        if final:
            m["gfin"] = np.ascontiguousarray(np.broadcast_to(gfin[None, :], (128, D)))
        in_maps.append(m)
    res = run_bass_kernel_spmd(nc, in_maps, core_ids=list(range(NCORE)))
    if final:
        return np.concatenate([res.results[c]["y"] for c in range(NCORE)], axis=0)
    h = np.concatenate([res.results[c]["hout"] for c in range(NCORE)], axis=0)
    xT = np.concatenate([res.results[c]["xT"] for c in range(NCORE)], axis=1)
    return h, xT


def _consts_nsa(ntok=S):
    c = {}
    c["ident"] = _bf(np.eye(128))
    sl = np.arange(128)[:, None]
    tl = np.arange(512)[None, :]
    cmi = np.stack([(sl + 128 * j <= tl) for j in range(4)]).astype(np.float32)
    c["cmneg"] = _bf(((cmi - 1.0) * MASKV).transpose(1, 0, 2))
    wm = np.stack([(sl + 128 * j > tl) for j in range(4)]).astype(np.float32)
    c["wmneg"] = _bf(((wm - 1.0) * MASKV).transpose(1, 0, 2))
    cp = np.stack([(16 * sl + 31 - 512 * r <= tl) for r in range(5)]).astype(np.float32)
    c["cpneg"] = _bf(((cp - 1.0) * MASKV).transpose(1, 0, 2))
    m = np.arange(128)[:, None]
    cc = np.arange(8192)[None, :]
    c["eall"] = _bf(np.where(cc // 64 == m, MASKV, 0.0))
    n = (np.arange(8)[None, :, None] * 128 + np.arange(128)[:, None, None])
    mm = np.arange(256)[None, None, :]
    ov = ((n >= 4 * mm - 1) & (n <= 4 * mm + 3)).astype(np.float32)
    c["ovl1"] = _bf(np.concatenate([np.ones((128, 8, 1), np.float32), ov], axis=2))
    half = 8
    inv_freq = (np.float32(500000.0) ** (-np.arange(half, dtype=np.float32) * np.float32(2.0) / np.float32(16)))
    ang = np.arange(ntok, dtype=np.float32)[:, None] * inv_freq[None, :].astype(np.float32)
    cs = np.cos(ang).astype(np.float32).T
    sn = np.sin(ang).astype(np.float32).T
    cos64 = np.ones((64, ntok), np.float32)
    sin64 = np.zeros((64, ntok), np.float32)
    cos64[0:8] = cs
    cos64[8:16] = cs
    sin64[0:8] = -sn
    sin64[8:16] = sn
    c["cosT"] = np.ascontiguousarray(np.concatenate([cos64, cos64], 0))
    c["sinT"] = np.ascontiguousarray(np.concatenate([sin64, sin64], 0))
    tq = np.arange(128)
    fc = np.zeros((128, 2), np.float32)
    fc[:, 0] = np.where(tq < 64, 4e6, 0.0)
    fc[:, 1] = np.where(tq < 64, -1.0, 3e6)
    c["fc"] = fc
    return c


L3_IN = (("w_kvr", [D, 128]), ("w_ksw", [D, 128]), ("w_kswp", [D, 128]), ("w_vsw", [D, 128]), ("w_qg", [D, 256]),
         ("w_q1", [D, 128]), ("w_q1p", [D, 128]), ("w_q2", [D, 128]), ("w_q2p", [D, 128]), ("w_g", [D, 6]),
         ("gb", [1, 6]), ("gkv", [128, 8]), ("gq", [128, 8]), ("cw1", [128, 32, 256]), ("cw2", [128, 2, 2, 64]),
         ("cposT", [128, 32]), ("fc", [128, 2]))
L3_CSHAPE = {"ident": [128, 128], "cmneg": [128, 4, 512], "wmneg": [128, 4, 512], "cpneg": [128, 5, 512],
             "eall": [128, 8192], "ovl1": [128, 8, 257]}


def l3_tensors(nc, ntok, pre=""):
    di = lambda name, shp, dt=F32: nc.dram_tensor(pre + name, shp, dt, kind="ExternalInput").ap()
    T = {n: di(n, shp) for n, shp in L3_IN}
    for n, shp in L3_CSHAPE.items():
        T[n] = di(n, shp, BF16)
    T["cosT"] = di("cosT", [128, ntok])
    T["sinT"] = di("sinT", [128, ntok])
    return T


def emit_L3(nc, k, sb, ps, T, ntok=S, dbg=False):
    NT = ntok // 128
    NG = ntok // 512
    NCMP = (ntok - 32) // 16 + 1
    NJ = (NCMP + 127) // 128
    NNG = (NCMP + 511) // 512
    (w_kvr, w_ksw, w_kswp, w_vsw, w_qg, w_q1, w_q1p, w_q2, w_q2p, w_g, gb, gkv, gq, cw1, cw2, cposT, fc) = (
        T[n] for n, _ in L3_IN)
    cshape = L3_CSHAPE
    cst = {n: T[n] for n in cshape}
    cosT, sinT, o = T["cosT"], T["sinT"], T["o"]
    xT_group = T["xT_group"]
    pre_rd = [b for b in (T.get("xT_b"),) if b is not None]
    REGION = 75776
    reg_base = (sb.top - REGION) // 64 * 64
    cur = {2: reg_base, 3: reg_base}

    def sbr(phase, name, shape, dtype):
        n = 1
        for d_ in shape[1:]:
            n *= d_
        nbytes = n * (4 if dtype == F32 else 2)
        off = cur[phase]
        cur[phase] = off + (nbytes + 63) // 64 * 64
        assert cur[phase] <= reg_base + REGION, (name, cur[phase] - reg_base)
        return nc.alloc_sbuf_tensor_at(name, shape, dtype, offset=off)

    bar = [None]

    def b3(name=""):
        b = k.buf(name)
        b.w = bar[0]
        return b

    c_sb = {"ident": sb("c_ident", cshape["ident"], BF16)}
    c_b = {"ident": k.buf("ident")}
    k.dma("sp", c_sb["ident"][:], cst["ident"], writes=[c_b["ident"]])
    fc_sb = sb("fc_sb", [128, 2], F32)
    fc_b = k.buf("fc")
    k.dma("sp", fc_sb[:], fc, writes=[fc_b])
    gkv_sb = sb("gkv_sb", [128, 8], F32)
    gq_sb = sb("gq_sb", [128, 8], F32)
    gkv_b = k.buf()
    gq_b = k.buf()
    k.dma("sp", gkv_sb[:], gkv, writes=[gkv_b])
    k.dma("sp", gq_sb[:], gq, writes=[gq_b])
    gb_sb = sb("gb_sb", [1, 6], F32)
    gb_b = k.buf()
    k.dma("sp", gb_sb[:], gb, writes=[gb_b])
    gbb = sb("gbb", [1, 6], BF16)
    gbb_b = k.buf()
    k.op("dve", lambda e: e.tensor_copy(gbb[:], gb_sb[:]), reads=[gb_b], writes=[gbb_b])
    ones1 = sb("ones1", [1, 128], BF16)
    ones1_b = k.buf()
    k.op("dve", lambda e: e.memset(ones1[:], 1.0), writes=[ones1_b])

    wst = sb("wst", [128, 8, 256], F32)
    wst_b = k.buf("wst")
    W = {}
    W_b = {}

    def load_w(name, ap, ncol, gsb, gbuf, scl):
        W[name] = sb("W_" + name, [128, 8, ncol], BF16)
        W_b[name] = k.buf("W" + name)
        k.dma("sp", wst[:, :, 0:ncol], ap.rearrange("(k p) n -> p k n", p=128), writes=[wst_b])
        for kk in range(8):
            k.op("dve", lambda e, kk=kk, name=name, scl=scl, gsb=gsb, ncol=ncol: e.tensor_scalar(
                W[name][:, kk, :], wst[:, kk, 0:ncol], gsb[:, kk:kk + 1], scl, ALU.mult, ALU.mult),
                reads=[wst_b, gbuf], writes=[W_b[name]])

    load_w("kvr", w_kvr, 128, gkv_sb, gkv_b, 1.0)
    load_w("ksw", w_ksw, 128, gkv_sb, gkv_b, 1.0)
    load_w("kswp", w_kswp, 128, gkv_sb, gkv_b, 1.0)
    load_w("vsw", w_vsw, 128, gkv_sb, gkv_b, 1.0)
    load_w("qg", w_qg, 256, gq_sb, gq_b, 0.125)
    load_w("q1", w_q1, 128, gq_sb, gq_b, 0.125)
    load_w("q1p", w_q1p, 128, gq_sb, gq_b, 0.125)
    load_w("q2", w_q2, 128, gq_sb, gq_b, 0.125)
    load_w("q2p", w_q2p, 128, gq_sb, gq_b, 0.125)
    load_w("g", w_g, 6, gq_sb, gq_b, 1.0)

    cw1st = sbr(2, "cw1st", [128, 8, 256], F32)
    cw1st_b = k.buf()
    cw1b = sbr(2, "cw1b", [128, 32, 256], BF16)
    cw1b_b = k.bufs(2, "cw1b")
    for qt in range(4):
        k.dma("sp", cw1st[:], cw1[:, qt * 8:(qt + 1) * 8, :], writes=[cw1st_b])
        k.op("dve", lambda e, qt=qt: e.tensor_copy(cw1b[:, qt * 8:(qt + 1) * 8, :], cw1st[:]),
             reads=[cw1st_b], writes=[cw1b_b[qt // 2]])
    cw2st = sb("cw2st", [128, 2, 2, 64], F32)
    cw2st_b = k.buf()
    k.dma("sp", cw2st[:], cw2, writes=[cw2st_b])
    cw2k = sb("cw2k", [128, 2, 128], BF16)
    cw2v = sb("cw2v", [128, 2, 64], BF16)
    cw2_b = k.buf()
    k.op("dve", lambda e: e.tensor_copy(cw2k[:, :, 0:64], cw2st[:, :, 0, :]), reads=[cw2st_b], writes=[cw2_b])
    k.op("dve", lambda e: e.tensor_copy(cw2k[:, :, 64:128], cw2st[:, :, 0, :]), reads=[cw2st_b], writes=[cw2_b])
    k.op("dve", lambda e: e.tensor_copy(cw2v[:], cw2st[:, :, 1, :]), reads=[cw2st_b], writes=[cw2_b])
    cpos_st = sb("cpos_st", [128, 32], F32)
    cpos_b = k.buf()
    k.dma("sp", cpos_st[:], cposT, writes=[cpos_b])
    cposb = sb("cposb", [128, 32], BF16)
    cposb_b = k.buf()
    k.op("dve", lambda e: e.tensor_copy(cposb[:], cpos_st[:]), reads=[cpos_b], writes=[cposb_b])

    RAW = sbr(2, "RAW", [128, ntok], BF16)
    RAW_b = k.bufs(NG, "RAW")
    KSW = sb("KSW", [128, ntok], BF16)
    KSW_b = k.bufs(NG, "KSW")
    VSW = sb("VSW", [128, NT, 132], BF16)
    VSW_b = k.bufs(NG, "VSW")
    vones_b = k.buf()
    k.op("pool", lambda e: e.memset(VSW[:, :, 64:65], 1.0), writes=[vones_b] + VSW_b)
    k.op("pool", lambda e: e.memset(VSW[:, :, 130:131], 1.0), writes=[vones_b] + VSW_b)

    xg = [sb("xg%d" % i, [128, 8, 512], BF16) for i in range(1)]
    xg_b = k.bufs(1, "xg")
    cs_sb = [sb("cs%d" % i, [128, 512], F32) for i in range(1)]
    sn_sb = [sb("sn%d" % i, [128, 512], F32) for i in range(1)]
    cs_b = k.bufs(1, "cs")
    sn_b = k.bufs(1, "sn")
    t1 = [sb("t1_%d" % i, [128, 512], F32) for i in range(1)]
    t2 = [sb("t2_%d" % i, [128, 512], F32) for i in range(1)]
    t1_b = k.bufs(1, "t1")
    t2_b = k.bufs(1, "t2")
    pS = [ps("pS%d" % i, [128, 512], F32) for i in range(3)]
    pS_b = k.bufs(3, "pS")
    nps = [0]

    def next_ps():
        i = nps[0] % 3
        nps[0] += 1
        return i


    def load_group(G, cnt):
        i = 0
        k.dma("sp", xg[i][:], xT_group(G), reads=pre_rd, writes=[xg_b[i]], sembuf=xg_b[i])
        k.dma("sp", cs_sb[i][:], cosT[:, G * 512:(G + 1) * 512], writes=[cs_b[i]])
        k.dma("sp", sn_sb[i][:], sinT[:, G * 512:(G + 1) * 512], writes=[sn_b[i]])
        return i

    def proj(wname, i, col0=0, ncol=128):
        pi = next_ps()
        for kk in range(8):
            k.op("pe", lambda e, pi=pi, kk=kk, wname=wname, i=i, col0=col0, ncol=ncol: e.matmul(
                pS[pi][0:ncol, :], W[wname][:, kk, col0:col0 + ncol], xg[i][:, kk, :],
                start=(kk == 0), stop=(kk == 7)),
                reads=[W_b[wname], xg_b[i]], writes=[pS_b[pi]])
        return pi

    def rope_to(dst_ap, dst_bufs, wname, wpname, i, ti):
        ti = 0
        pa = proj(wname, i)
        pb = proj(wpname, i)
        k.op("dve", lambda e, pa=pa, i=i, ti=ti: e.tensor_tensor(t1[ti][:], pS[pa][:], cs_sb[i][:], ALU.mult),
             reads=[pS_b[pa], cs_b[i]], writes=[t1_b[ti]])
        k.op("dve", lambda e, pb=pb, i=i, ti=ti: e.tensor_tensor(t2[ti][:], pS[pb][:], sn_sb[i][:], ALU.mult),
             reads=[pS_b[pb], sn_b[i]], writes=[t2_b[ti]])
        k.op("pool", lambda e, ti=ti, dst_ap=dst_ap: e.tensor_tensor(dst_ap, t1[ti][:], t2[ti][:], ALU.add),
             reads=[t1_b[ti], t2_b[ti]], writes=dst_bufs)

    cnt = 0
    for G in range(NG):
        i = load_group(G, cnt)
        cnt += 1
        pi = proj("kvr", i)
        k.op("act", lambda e, pi=pi, G=G: e.activation(out=RAW[:, G * 512:(G + 1) * 512], in_=pS[pi][:], func=AF.Copy),
             reads=[pS_b[pi]], writes=[RAW_b[G]])
        rope_to(KSW[:, G * 512:(G + 1) * 512], [KSW_b[G]], "ksw", "kswp", i, G % 2)
        pi = next_ps()
        for tt in range(4):
            for kk in range(8):
                k.op("pe", lambda e, pi=pi, kk=kk, tt=tt, i=i: e.matmul(
                    pS[pi][:, tt * 128:(tt + 1) * 128], xg[i][:, kk, tt * 128:(tt + 1) * 128], W["vsw"][:, kk, :],
                    start=(kk == 0), stop=(kk == 7)),
                    reads=[W_b["vsw"], xg_b[i]], writes=[pS_b[pi]])
        pv = pS[pi][:].rearrange("p (a b) -> p a b", a=4)
        k.op("act", lambda e, pv=pv, G=G: e.activation(out=VSW[:, G * 4:(G + 1) * 4, 0:64], in_=pv[:, :, 0:64], func=AF.Copy),
             reads=[pS_b[pi]], writes=[VSW_b[G]])
        k.op("act", lambda e, pv=pv, G=G: e.activation(out=VSW[:, G * 4:(G + 1) * 4, 66:130], in_=pv[:, :, 64:128], func=AF.Copy),
             reads=[pS_b[pi]], writes=[VSW_b[G]])

    KCT = sb("KCT", [128, NJ * 128], BF16)
    KCT_b = k.buf("KCT")
    RC = sb("RC", [128, NJ, 322], BF16)[:, :, 0:321]
    RC_b = k.buf("RC")
    k.op("pool", lambda e: e.memset(KCT[:], 0.0), writes=[KCT_b])
    k.op("pool", lambda e: e.memset(RC[:, :, 0:64], 0.0), writes=[RC_b])
    c_sb["ovl1"] = sb("c_ovl1", cshape["ovl1"], BF16)
    c_b["ovl1"] = k.buf("ovl1")
    k.dma("sp", c_sb["ovl1"][:], cst["ovl1"], writes=[c_b["ovl1"]])
    k.op("pool", lambda e: e.tensor_copy(RC[:, :, 64:321], c_sb["ovl1"][:, 0:NJ, :]), reads=[c_b["ovl1"]], writes=[RC_b])
    posb = sb("posb", [128, 2, 2], F32)
    posb_b = k.buf()
    pM = ps("pM", [128, 512], F32)
    pP = pM[:, 0:4]
    pP_b = k.buf()
    for c2 in range(2):
        for ch in range(2):
            for l in range(32):
                k.op("pe", lambda e, c2=c2, ch=ch, l=l: e.matmul(
                    pP[:, c2 * 2 + ch: c2 * 2 + ch + 1], cw1b[64 * c2:64 * c2 + 64, l, ch * 128:(ch + 1) * 128],
                    cposb[64 * c2:64 * c2 + 64, l:l + 1], start=(l == 0), stop=(l == 31)),
                    reads=[cw1b_b[l // 16], cposb_b], writes=[pP_b])
    for c2 in range(2):
        for ch in range(2):
            k.op("dve", lambda e, c2=c2, ch=ch: e.tensor_copy(posb[:, ch, c2:c2 + 1], pP[:, c2 * 2 + ch: c2 * 2 + ch + 1]),
                 reads=[pP_b], writes=[posb_b])
    hidT = sbr(2, "hidT", [128, 2, 2, 1024], BF16)
    hid_b = [[k.buf() for _ in range(2)] for _ in range(2)]
    k.op("pool", lambda e: e.memset(hidT[:], 0.0), writes=[hid_b[0][0], hid_b[0][1], hid_b[1][0], hid_b[1][1]])
    gu = [sbr(2, "gu%d" % i, [128, 512], F32) for i in range(2)]
    gt = [sbr(2, "gt%d" % i, [128, 512], F32) for i in range(2)]
    gu_b = k.bufs(2, "gu")
    gt_b = k.bufs(2, "gt")
    gi = 0
    for c2 in range(2):
        cp = slice(64 * c2, 64 * c2 + 64)
        for ch in range(2):
            for ng in range(NNG):
                n0 = ng * 512
                nn = min(512, NCMP - n0)
                pi = next_ps()
                for l in range(32):
                    k.op("pe", lambda e, pi=pi, l=l, cp=cp, ch=ch, n0=n0, nn=nn: e.matmul(
                        pS[pi][:, 0:nn], cw1b[cp, l, ch * 128:(ch + 1) * 128],
                        RAW[cp, 16 * n0 + l: 16 * n0 + l + 16 * (nn - 1) + 1: 16],
                        start=(l == 0), stop=(l == 31)),
                        reads=[cw1b_b[l // 16]] + RAW_b, writes=[pS_b[pi]])
                ui = gi % 2
                gi += 1
                k.op("act", lambda e, pi=pi, ui=ui, nn=nn, ch=ch, c2=c2: e.activation(
                    out=gu[ui][:, 0:nn], in_=pS[pi][:, 0:nn], func=AF.Identity, bias=posb[:, ch, c2:c2 + 1]),
                    reads=[pS_b[pi], posb_b], writes=[gu_b[ui]])
                k.op("dve", lambda e, ui=ui, nn=nn: e.tensor_tensor(gt[ui][:, 0:nn], gu[ui][:, 0:nn], gu[ui][:, 0:nn], ALU.mult),
                     reads=[gu_b[ui]], writes=[gt_b[ui]])
                k.op("dve", lambda e, ui=ui, nn=nn: e.tensor_scalar(gt[ui][:, 0:nn], gt[ui][:, 0:nn], 0.044715, 1.0, ALU.mult, ALU.add),
                     reads=[gt_b[ui]], writes=[gt_b[ui]])
                k.op("dve", lambda e, ui=ui, nn=nn: e.tensor_tensor(gt[ui][:, 0:nn], gt[ui][:, 0:nn], gu[ui][:, 0:nn], ALU.mult),
                     reads=[gt_b[ui], gu_b[ui]], writes=[gt_b[ui]])
                k.op("act", lambda e, ui=ui, nn=nn: e.activation(
                    out=gt[ui][:, 0:nn], in_=gt[ui][:, 0:nn], func=AF.Sigmoid, scale=1.5957691216057308),
                    reads=[gt_b[ui]], writes=[gt_b[ui]])
                k.op("dve", lambda e, ui=ui, nn=nn, c2=c2, ch=ch, n0=n0: e.tensor_tensor(
                    hidT[:, c2, ch, n0:n0 + nn], gt[ui][:, 0:nn], gu[ui][:, 0:nn], ALU.mult),
                    reads=[gt_b[ui], gu_b[ui]], writes=[hid_b[c2][ch]])
    for ng in range(NNG):
        n0 = ng * 512
        nn = min(512, NCMP - n0)
        pi = next_ps()
        for ch in range(2):
            k.op("pe", lambda e, pi=pi, ch=ch, n0=n0, nn=nn: e.matmul(
                pS[pi][:, 0:nn], cw2k[:, ch, :], hidT[:, 0, ch, n0:n0 + nn], start=(ch == 0), stop=(ch == 1)),
                reads=[cw2_b, hid_b[0][ch]], writes=[pS_b[pi]])
        k.op("act", lambda e, pi=pi, n0=n0, nn=nn: e.activation(out=KCT[:, n0:n0 + nn], in_=pS[pi][:, 0:nn], func=AF.Copy),
             reads=[pS_b[pi]], writes=[KCT_b])
    for j in range(NJ):
        nn = min(128, NCMP - 128 * j)
        pi = next_ps()
        for ch in range(2):
            k.op("pe", lambda e, pi=pi, ch=ch, j=j, nn=nn: e.matmul(
                pS[pi][0:nn, 0:64], hidT[:, 1, ch, 128 * j:128 * j + nn], cw2v[:, ch, :], start=(ch == 0), stop=(ch == 1)),
                reads=[cw2_b, hid_b[1][ch]], writes=[pS_b[pi]])
        k.op("act", lambda e, pi=pi, j=j, nn=nn: e.activation(out=RC[0:nn, j, 0:64], in_=pS[pi][0:nn, 0:64], func=AF.Copy),
             reads=[pS_b[pi]], writes=[RC_b])

    bar[0] = ("c", "act", len(k.ops["act"]) - 1)
    for n in ("eall", "cmneg", "wmneg", "cpneg"):
        c_sb[n] = sbr(3, "c_" + n, cshape[n], BF16)
        c_b[n] = b3(n)
        k.dma("sp", c_sb[n][:], cst[n], writes=[c_b[n]])
    QC = [sbr(3, "QC%d" % i, [128, 2, 512], BF16) for i in range(2)]
    QC_b = [b3() for _ in range(2)]
    QR = [sbr(3, "QR%d" % i, [128, 2, 512], BF16) for i in range(2)]
    QR_b = [[b3() for _ in range(2)] for i in range(2)]
    GT = [sb("GT%d" % i, [128, 4, 6], F32) for i in range(2)]
    GT_b = k.bufs(2, "GT")
    pG = pM[:, 8:40].rearrange("p (a b) -> p a b", a=4)
    pG_b = k.buf()
    eC = [sbr(3, "eC%d" % i, [128, NJ, 512], BF16) for i in range(2)]
    eC_b = [b3() for _ in range(2)]
    pCbank = [ps("pC%d" % i, [128, 512], F32) for i in range(2)]
    pC = [pb[:, 0:321] for pb in pCbank]
    pCv = [pb[:, 0:260].rearrange("p (a b) -> p a b", a=4) for pb in pCbank]
    pC_b = k.bufs(2, "pC")
    rsum = [sb("rsum%d" % i, [128, 1], F32) for i in range(4)]
    rsum_b = k.bufs(4, "rsum")
    ocmp = sb("ocmp", [128, 4, 2, 64], F32)
    ocmp_b = k.bufs(4, "ocmp")
    imp = sbr(3, "imp", [128, 4, 256], F32)
    imp_b = [b3() for _ in range(4)]
    m8a = sb("m8a", [128, 8], F32)
    m8b = sb("m8b", [128, 8], F32)
    m8_b = k.buf()
    imw = sb("imw", [128, 256], F32)
    imw_b = k.buf()
    selm = sb("selm", [128, 256], BF16)
    selm_b = k.buf()
    pTs = pM[:, 128:256].bitcast(BF16).rearrange("p (a b) -> p a b", a=2)
    pTs_b = k.buf()
    selT = [sbr(3, "selT%d" % i, [128, 2, 512], BF16) for i in range(2)]
    selT_b = [b3() for _ in range(2)]
    eS = [sbr(3, "eS%d" % i, [128, 512], BF16) for i in range(3)]
    eS_b = [b3() for _ in range(3)]
    pO = [ps("pOb%d" % i, [128, 512], F32) for i in range(2)]
    oTs = [sbr(3, "oTs%d" % i, [65, 512], F32) for i in range(2)]
    oTs_b = [b3() for _ in range(2)]
    id32 = sbr(3, "id32", [128, 128], F32)
    id32_b = b3()
    k.op("dve", lambda e: e.tensor_copy(id32[:], c_sb["ident"][:]), reads=[c_b["ident"]], writes=[id32_b])
    selm4 = [sbr(3, "selm4_%d" % i, [128, 256], BF16) for i in range(4)]
    selm4_b = [b3() for _ in range(4)]
    ntl3 = [0]
    pO_b = k.bufs(2, "pO")
    oacc = sb("oacc", [128, 4, 2, 64], F32)
    oacc_b = k.bufs(4, "oacc")
    rs2 = [sb("rs2_%d" % i, [128, 4, 1], F32) for i in range(2)]
    rs2_b = k.bufs(2, "rs2")
    osb = [sb("osb%d" % i, [128, 4, 128], BF16) for i in range(2)]
    osb_b = k.bufs(2, "osb")
    o_v = o.rearrange("(t p) f -> p t f", p=128)
    nes = [0]
    npo = [0]

    for G in range(NG):
        i = load_group(G, cnt)
        cnt += 1
        gi2 = G % 2
        r = G % 4
        nj = G // 4 + 1
        qs = slice(G * 512, (G + 1) * 512)
        for hh in range(2):
            pi = proj("qg", i, col0=128 * hh)
            k.op("act", lambda e, pi=pi, hh=hh, gi2=gi2: e.activation(out=QC[gi2][:, hh, :], in_=pS[pi][:], func=AF.Copy),
                 reads=[pS_b[pi]], writes=[QC_b[gi2]])
        rope_to(QR[gi2][:, 0, :], [QR_b[gi2][0]], "q1", "q1p", i, 0)
        rope_to(QR[gi2][:, 1, :], [QR_b[gi2][1]], "q2", "q2p", i, 1)
        for tt in range(4):
            for kk in range(8):
                k.op("pe", lambda e, kk=kk, tt=tt, i=i: e.matmul(
                    pG[:, tt, 0:6], xg[i][:, kk, tt * 128:(tt + 1) * 128], W["g"][:, kk, :], start=(kk == 0), stop=False),
                    reads=[W_b["g"], xg_b[i]], writes=[pG_b])
            k.op("pe", lambda e, tt=tt: e.matmul(pG[:, tt, 0:6], ones1[:], gbb[:], start=False, stop=True),
                 reads=[ones1_b, gbb_b], writes=[pG_b])
        k.op("act", lambda e, gi2=gi2: e.activation(out=GT[gi2][:], in_=pG[:, :, 0:6], func=AF.Sigmoid),
             reads=[pG_b], writes=[GT_b[gi2]])

        for h4 in range(4):
            ei = h4 % 2
            hp = slice(64 * (h4 % 2), 64 * (h4 % 2) + 64)
            for j in range(nj):
                pi = next_ps()
                msk = None
                if j == nj - 1:
                    msk = r
                elif j == nj - 2 and r == 0:
                    msk = 4
                k.op("pe", lambda e, pi=pi, hp=hp, j=j, gi2=gi2, h4=h4, msk=msk: e.matmul(
                    pS[pi][:], KCT[hp, 128 * j:128 * j + 128], QC[gi2][hp, h4 // 2, :], start=True, stop=(msk is None)),
                    reads=[KCT_b, QC_b[gi2]], writes=[pS_b[pi]])
                if msk is not None:
                    k.op("pe", lambda e, pi=pi, msk=msk: e.matmul(
                        pS[pi][:], c_sb["ident"][:], c_sb["cpneg"][:, msk, :], start=False, stop=True),
                        reads=[c_b["ident"], c_b["cpneg"]], writes=[pS_b[pi]])
                k.op("act", lambda e, pi=pi, ei=ei, j=j: e.activation(out=eC[ei][:, j, :], in_=pS[pi][:], func=AF.Exp),
                     reads=[pS_b[pi]], writes=[eC_b[ei]])
            own = (h4 // 2 == 0)
            for tc in range(4):
                ci = (h4 * 4 + tc) % 2
                for j in range(nj):
                    k.op("pe", lambda e, ci=ci, ei=ei, j=j, tc=tc, nj=nj: e.matmul(
                        pC[ci], eC[ei][:, j, tc * 128:(tc + 1) * 128], RC[:, j, :], start=(j == 0), stop=(j == nj - 1)),
                        reads=[eC_b[ei], RC_b], writes=[pC_b[ci]])
                ri = (h4 * 4 + tc) % 4
                k.op("dve", lambda e, ci=ci, ri=ri: e.tensor_scalar(rsum[ri][:], pC[ci][:, 64:65], 1e-30, None, ALU.add),
                     reads=[pC_b[ci]], writes=[rsum_b[ri]])
                k.op("dve", lambda e, ri=ri: e.reciprocal(rsum[ri][:], rsum[ri][:]),
                     reads=[rsum_b[ri]], writes=[rsum_b[ri]])
                if h4 == 0:
                    k.op("dve", lambda e, ci=ci, ri=ri, tc=tc: e.tensor_scalar(
                        imp[:, tc, :], pC[ci][:, 65:321], rsum[ri][:, 0:1], None, ALU.mult),
                        reads=[pC_b[ci], rsum_b[ri]], writes=[imp_b[tc]])
                else:
                    k.op("dve", lambda e, ci=ci, ri=ri, tc=tc: e.scalar_tensor_tensor(
                        imp[:, tc, :], pC[ci][:, 65:321], rsum[ri][:, 0:1], imp[:, tc, :], ALU.mult, ALU.add),
                        reads=[pC_b[ci], rsum_b[ri], imp_b[tc]], writes=[imp_b[tc]])
                if own:
                    k.op("dve", lambda e, ri=ri, gi2=gi2, tc=tc, h4=h4: e.tensor_tensor(
                        rsum[ri][:], rsum[ri][:], GT[gi2][:, tc, h4:h4 + 1], ALU.mult),
                        reads=[rsum_b[ri], GT_b[gi2]], writes=[rsum_b[ri]])
                    k.op("dve", lambda e, ci=ci, ri=ri, tc=tc, h4=h4: e.tensor_scalar(
                        ocmp[:, tc, h4, :], pC[ci][:, 0:64], rsum[ri][:, 0:1], None, ALU.mult),
                        reads=[pC_b[ci], rsum_b[ri]], writes=[ocmp_b[tc]])

        si = G % 2
        topk_tcs = []
        for tc in range(4):
            Wd = 8 * G + 2 * tc + 2
            if Wd <= 16:
                k.op("pool", lambda e, si=si, tc=tc: e.memset(selT[si][:, :, tc * 128:(tc + 1) * 128], 0.0),
                     writes=[selT_b[si]])
                continue
            topk_tcs.append(tc)
            k.op("dve", lambda e, tc=tc, Wd=Wd: e.tensor_copy(imw[:, 0:Wd], imp[:, tc, 0:Wd]),
                 reads=[imp_b[tc]], writes=[imw_b])
            k.op("dve", lambda e: e.memset(imw[:, 0:1], 1e6), writes=[imw_b])
            k.op("dve", lambda e, Wd=Wd: e.memset(imw[:, Wd - 2:Wd - 1], 2e6), writes=[imw_b])
            k.op("dve", lambda e, Wd=Wd: e.tensor_copy(imw[:, Wd - 1:Wd], fc_sb[:, 1:2]), reads=[fc_b], writes=[imw_b])
            k.op("dve", lambda e, Wd=Wd: e.tensor_tensor(imw[:, Wd - 3:Wd - 2], imw[:, Wd - 3:Wd - 2], fc_sb[:, 0:1], ALU.max),
                 reads=[fc_b, imw_b], writes=[imw_b])
            k.op("dve", lambda e, Wd=Wd: e.max(out=m8a[:], in_=imw[:, 0:Wd]), reads=[imw_b], writes=[m8_b])
            k.op("dve", lambda e, tc=tc, Wd=Wd: e.match_replace(
                out=imp[:, tc, 0:Wd], in_to_replace=m8a[:], in_values=imw[:, 0:Wd], imm_value=-2.0),
                reads=[imw_b, m8_b], writes=[imp_b[tc]])
            k.op("dve", lambda e, tc=tc, Wd=Wd: e.max(out=m8b[:], in_=imp[:, tc, 0:Wd]), reads=[imp_b[tc]], writes=[m8_b])
            k.op("dve", lambda e, tc=tc: e.memset(selm4[tc][:], 0.0), writes=[selm4_b[tc]])
            k.op("dve", lambda e, tc=tc, Wd=Wd: e.tensor_scalar(
                selm4[tc][:, 0:Wd], imw[:, 0:Wd], m8b[:, 7:8], 1.0, ALU.is_ge, ALU.subtract),
                reads=[imw_b, m8_b], writes=[selm4_b[tc]])

        def emit_selT(si=si):
            for tc in topk_tcs:
                for mh in range(2):
                    k.op("pe", lambda e, mh=mh, tc=tc: e.transpose(
                        pTs[:, mh, :], selm4[tc][:, mh * 128:(mh + 1) * 128], c_sb["ident"][:]),
                        reads=[selm4_b[tc], c_b["ident"]], writes=[pTs_b])
                k.op("act", lambda e, tc=tc: e.activation(
                    out=selT[si][:, :, tc * 128:(tc + 1) * 128], in_=pTs, func=AF.Copy),
                    reads=[pTs_b], writes=[selT_b[si]])

        seqs = []
        for br in (1, 0):
            for hi in range(2):
                if br == 0:
                    kbs = list(range(0, 4 * G + 4))
                else:
                    kbs = list(range(max(0, 4 * G - 4), 4 * G + 4))
                seqs.append((hi, br, kbs))
        tl3 = []
        for sq, (hi, br, kbs) in enumerate(seqs):
            for step, kb in enumerate(kbs):
                tl3.append((len(tl3), sq, hi, br, kb, step, len(kbs)))
        sel_done = [False]

        def T1(t, si=si, gi2=gi2, G=G):
            i, sq, hi, br, kb, step, nk = t
            if br == 0 and not sel_done[0]:
                emit_selT()
                sel_done[0] = True
            pi = (ntl3[0] + i) % 3
            if br == 0:
                qv = QR[gi2][0:64, hi, :]
                qb = QR_b[gi2][hi]
                kp = slice(0, 64)
            else:
                qv = QR[gi2][64:128, 1 - hi, :]
                qb = QR_b[gi2][1 - hi]
                kp = slice(64, 128)
            jd = kb - 4 * G
            extra = []
            if br == 0:
                extra.append(("sel", kb))
            if jd >= 0:
                extra.append(("cm", jd))
            if br == 1 and kb - (4 * G - 4) < 4 and 4 * G - 4 >= 0:
                extra.append(("wm", kb - (4 * G - 4)))
            k.op("pe", lambda e: e.matmul(
                pS[pi][:], KSW[kp, 128 * kb:128 * kb + 128], qv, start=True, stop=(len(extra) == 0)),
                reads=[KSW_b[kb // 4], qb], writes=[pS_b[pi]])
            for xi, (kind, a) in enumerate(extra):
                last = (xi == len(extra) - 1)
                if kind == "sel":
                    k.op("pe", lambda e, a=a, last=last: e.matmul(
                        pS[pi][:], c_sb["eall"][:, (a % 64) * 128:(a % 64) * 128 + 128], selT[si][:, a // 64, :],
                        start=False, stop=last),
                        reads=[c_b["eall"], selT_b[si]], writes=[pS_b[pi]])
                else:
                    cname = "cmneg" if kind == "cm" else "wmneg"
                    k.op("pe", lambda e, a=a, cname=cname, last=last: e.matmul(
                        pS[pi][:], c_sb["ident"][:], c_sb[cname][:, a, :], start=False, stop=last),
                        reads=[c_b["ident"], c_b[cname]], writes=[pS_b[pi]])
            k.op("act", lambda e: e.activation(out=eS[pi][:], in_=pS[pi][:], func=AF.Exp),
                 reads=[pS_b[pi]], writes=[eS_b[pi]])

        def T2(t, si=si, gi2=gi2, G=G):
            i, sq, hi, br, kb, step, nk = t
            pi = (ntl3[0] + i) % 3
            po = sq % 2
            vc = slice(0, 65) if br == 0 else slice(66, 131)
            k.op("pe", lambda e: e.matmul(
                pO[po][0:65, :], VSW[:, kb, vc], eS[pi][:], start=(step == 0), stop=(step == nk - 1)),
                reads=[eS_b[pi], VSW_b[kb // 4]], writes=[pO_b[po]])
            if step < nk - 1:
                return
            ci = sq % 2
            k.op("act", lambda e: e.activation(out=oTs[po][:], in_=pO[po][0:65, :], func=AF.Copy),
                 reads=[pO_b[po]], writes=[oTs_b[po]])
            pcv = pCv[ci]
            for tc in range(4):
                k.op("pe", lambda e, tc=tc: e.transpose(
                    pcv[:, tc, :], oTs[po][:, tc * 128:(tc + 1) * 128], id32[0:65, 0:65]),
                    reads=[oTs_b[po], id32_b], writes=[pC_b[ci]])
            if dbg and G == 0 and sq == 0:
                d1 = sb("d1", [128, 260], F32)
                d1_b = k.buf()
                k.op("dve", lambda e: e.tensor_copy(d1[:].rearrange("p (a b) -> p a b", a=4), pcv), reads=[pC_b[ci]], writes=[d1_b])
                d1o = nc.dram_tensor("d1o", [128, 260], F32, kind="ExternalOutput").ap()
                k.dma("sp", d1o, d1[:], reads=[d1_b])
                d2o = nc.dram_tensor("d2o", [65, 512], F32, kind="ExternalOutput").ap()
                k.dma("sp", d2o, oTs[po][:], reads=[oTs_b[po]])
                k.wait("sp", [d1_b, oTs_b[po]])
            ri = po
            gcol = 2 * (br + 1) + hi
            k.op("dve", lambda e: e.reciprocal(rs2[ri][:], pcv[:, :, 64:65]),
                 reads=[pC_b[ci]], writes=[rs2_b[ri]])
            k.op("dve", lambda e: e.tensor_tensor(rs2[ri][:], rs2[ri][:], GT[gi2][:, :, gcol:gcol + 1], ALU.mult),
                 reads=[rs2_b[ri], GT_b[gi2]], writes=[rs2_b[ri]])
            for tc in range(4):
                prev = ocmp[:, tc, hi, :] if br == 1 else oacc[:, tc, hi, :]
                prev_b = ocmp_b[tc] if br == 1 else oacc_b[tc]
                k.op("dve", lambda e, tc=tc, prev=prev: e.scalar_tensor_tensor(
                    oacc[:, tc, hi, :], pcv[:, tc, 0:64], rs2[ri][:, tc, :], prev, ALU.mult, ALU.add),
                    reads=[pC_b[ci], rs2_b[ri], prev_b, oacc_b[tc]], writes=[oacc_b[tc]])

        NT3 = len(tl3)
        for n in range(-1, NT3):
            if n + 1 < NT3:
                T1(tl3[n + 1])
            if n >= 0:
                T2(tl3[n])
        ntl3[0] += NT3
        oi = G % 2
        k.op("pool", lambda e, oi=oi: e.tensor_copy(osb[oi][:].rearrange("p a (h d) -> p a h d", h=2), oacc[:]),
             reads=oacc_b, writes=[osb_b[oi]])
        k.dma("sp", o_v[:, 4 * G:4 * G + 4, :], osb[oi][:], reads=[osb_b[oi]])
    assert sb.cur <= reg_base, (sb.cur, reg_base)
    return osb_b


def build_L3(ntok=S, dbg=False):
    nc = bass.Bass("TRN2", target_bir_lowering=False)
    T = l3_tensors(nc, ntok)
    xT = nc.dram_tensor("xT", [D, ntok], BF16, kind="ExternalInput").ap()
    xT_v = xT.rearrange("(k p) t -> p k t", p=128)
    T["xT_group"] = lambda G: xT_v[:, :, G * 512:(G + 1) * 512]
    T["o"] = nc.dram_tensor("o", [ntok, 128], BF16, kind="ExternalOutput").ap()
    k = K(nc)
    w = emit_L3(nc, k, SBAlloc(nc), PSAlloc(nc), T, ntok, dbg)
    k.wait("sp", w)
    k.emit()
    return nc


def _rope_perm(wb):
    p = np.zeros_like(wb)
    p[:, 0:8] = wb[:, 8:16]
    p[:, 8:16] = wb[:, 0:8]
    return p


def l3_inputs(c, xT, nsa_w_kv, nsa_w_q, nsa_gate_b, kv_norm, g10, cmp_pos, cmp_w1, cmp_w2, cst):
    gq_ = c // 2
    hA, hB = 2 * c, 2 * c + 1
    partner = [h for h in range(4 * gq_, 4 * gq_ + 4) if h not in (hA, hB)]

    def kvc(br, kv):
        c0 = ((br * 2 + kv) * 4 + gq_) * 64
        return nsa_w_kv[:, c0:c0 + 64]

    def qc(h):
        return nsa_w_q[:, h * 64:(h + 1) * 64]

    cat = lambda *a: np.ascontiguousarray(np.concatenate(a, axis=1))
    gcols = [1024 + cc * 16 + h for cc in range(3) for h in (hA, hB)]
    m = {
        "xT": xT,
        "w_kvr": cat(kvc(0, 0), kvc(0, 1)),
        "w_ksw": cat(kvc(1, 0), kvc(2, 0)),
        "w_kswp": cat(_rope_perm(kvc(1, 0)), _rope_perm(kvc(2, 0))),
        "w_vsw": cat(kvc(1, 1), kvc(2, 1)),
        "w_qg": cat(qc(hA), qc(hB), qc(partner[0]), qc(partner[1])),
        "w_q1": cat(qc(hA), qc(hB)),
        "w_q1p": cat(_rope_perm(qc(hA)), _rope_perm(qc(hB))),
        "w_q2": cat(qc(hB), qc(hA)),
        "w_q2p": cat(_rope_perm(qc(hB)), _rope_perm(qc(hA))),
        "w_g": np.ascontiguousarray(nsa_w_q[:, gcols]),
        "gb": np.ascontiguousarray(nsa_gate_b[[cc * 16 + h for cc in range(3) for h in (hA, hB)]][None, :]),
        "gkv": _g_layout(kv_norm),
        "gq": _g_layout(g10),
        "cw1": np.ascontiguousarray(cmp_w1.reshape(2, 32, 64, 256).transpose(0, 2, 1, 3).reshape(128, 32, 256)),
        "cw2": np.ascontiguousarray(cmp_w2.reshape(2, 2, 128, 64).transpose(2, 1, 0, 3)),
        "cposT": np.ascontiguousarray(cmp_pos.transpose(0, 2, 1).reshape(128, 32)),
    }
    m.update(cst)
    return m


def run_L3(xT, nsa_w_kv, nsa_w_q, nsa_gate_b, kv_norm, g10, cmp_pos, cmp_w1, cmp_w2, ntok=S):
    nc = build_L3(ntok)
    cst = _consts_nsa(ntok)
    in_maps = [l3_inputs(c, xT, nsa_w_kv, nsa_w_q, nsa_gate_b, kv_norm, g10, cmp_pos, cmp_w1, cmp_w2, cst)
               for c in range(NCORE)]
    res = run_bass_kernel_spmd(nc, in_maps, core_ids=list(range(NCORE)))
    return np.concatenate([res.results[c]["o"] for c in range(NCORE)], axis=1)


def build_fused(upto=4):
    nc = bass.Bass("TRN2", target_bir_lowering=False)
    k = K(nc)
    sb = SBAlloc(nc)
    ps = PSAlloc(nc)
    T1 = l1_tensors(nc, S, pre="a_")
    T2 = tok_tensors(nc, False, pre="b_")
    if upto > 2:
        T3 = l3_tensors(nc, S, pre="c_")
    if upto > 3:
        T4 = tok_tensors(nc, True, pre="d_")
    xown = nc.dram_tensor("xown", [2048, D], F32, kind="ExternalInput").ap()
    if upto > 3:
        y = nc.dram_tensor("y", [2048, D], F32, kind="ExternalOutput").ap()
    o1_loc = nc.dram_tensor("o1_loc", [S, 128], BF16).ap()
    o1_all = nc.dram_tensor("o1_all", [8 * S, 128], BF16).ap()
    h1_loc = nc.dram_tensor("h1_loc", [2048, D], F32).ap()
    xT_loc = nc.dram_tensor("xT_loc", [D, 2048], BF16).ap()
    xT_all = nc.dram_tensor("xT_all", [8 * D, 2048], BF16).ap()
    o3_loc = nc.dram_tensor("o3_loc", [S, 128], BF16).ap()
    o3_all = nc.dram_tensor("o3_all", [8 * S, 128], BF16).ap()

    T1["o"] = o1_loc
    w = emit_L1(nc, k, sb, ps, T1, S)
    o1_b = k.buf("o1_all")
    k.collective(o1_loc, o1_all, w, o1_b)
    k.barrier()
    sb.reset()
    ps.reset()
    if upto == 2:
        h1_loc = nc.dram_tensor("dbg_h1", [2048, D], F32, kind="ExternalOutput").ap()
        xT_loc = nc.dram_tensor("dbg_xT", [D, 2048], BF16, kind="ExternalOutput").ap()
    T2.update(hin=xown, oall=o1_all, oall_b=o1_b, hout=h1_loc, xT=xT_loc)
    w = emit_tok(nc, k, sb, ps, T2, False)
    if upto == 2:
        k.wait("sp", w)
        k.emit()
        return nc
    xT_b = k.buf("xT_all")
    k.collective(xT_loc, xT_all, w, xT_b)
    k.barrier()
    sb.reset()
    ps.reset()
    T3["xT_group"] = lambda G: xT_all[(G // 4) * D:(G // 4 + 1) * D, (G % 4) * 512:(G % 4 + 1) * 512].rearrange(
        "(k p) t -> p k t", p=128)
    T3["xT_b"] = xT_b
    if upto == 3:
        o3_loc = nc.dram_tensor("dbg_o3", [S, 128], BF16, kind="ExternalOutput").ap()
    T3["o"] = o3_loc
    w = emit_L3(nc, k, sb, ps, T3, S)
    if upto == 3:
        k.wait("sp", w)
        k.emit()
        return nc
    o3_b = k.buf("o3_all")
    k.collective(o3_loc, o3_all, w, o3_b)
    k.barrier()
    sb.reset()
    ps.reset()
    T4.update(hin=h1_loc, oall=o3_all, oall_b=o3_b, y=y)
    w = emit_tok(nc, k, sb, ps, T4, True)
    k.wait("sp", w)
    k.emit()
    return nc


def kernel(**inp):
    f = lambda name: np.asarray(inp[name], dtype=np.float32)
    x = np.ascontiguousarray(f("x")[0])
    g = f("norm_gain")
    sb_w_qkv = f("sb_w_qkv")[0]
    nsa_w_kv, nsa_w_q, nsa_gate_b = f("nsa_w_kv"), f("nsa_w_q")[0], f("nsa_gate_b")[0]
    cmp_pos, cmp_w1, cmp_w2 = f("cmp_pos"), f("cmp_w1"), f("cmp_w2")
    nc = build_fused()
    c1 = _consts_attn()
    c3 = _consts_nsa(S)
    ident = _bf(np.eye(128))
    gfin = np.ascontiguousarray(np.broadcast_to(f("final_norm")[None, :], (128, D)))
    in_maps = []
    for c in range(NCORE):
        m = {"a_x": x, "a_g": _g_layout(g[0, 0]), "xown": np.ascontiguousarray(x[2048 * c:2048 * (c + 1)])}
        for i, nm in enumerate(("wq", "wk", "wv")):
            m["a_" + nm] = np.ascontiguousarray(sb_w_qkv[:, i * 1024 + 128 * c: i * 1024 + 128 * c + 128])
        for n, v in c1.items():
            m["a_" + n] = v
        sel = _sel_const(c)
        m.update({"b_sel": sel, "b_wo": f("sb_w_o")[0], "b_w1": f("mlp_w1")[0], "b_w2": f("mlp_w2")[0],
                  "b_g": _g_layout(g[0, 1]), "b_ident": ident})
        m3 = l3_inputs(c, None, nsa_w_kv, nsa_w_q, nsa_gate_b, f("kv_norm"), g[1, 0], cmp_pos, cmp_w1, cmp_w2, c3)
        for n, v in m3.items():
            if n != "xT":
                m["c_" + n] = v
        m.update({"d_sel": sel, "d_wo": f("nsa_w_o")[0], "d_w1": f("mlp_w1")[1], "d_w2": f("mlp_w2")[1],
                  "d_g": _g_layout(g[1, 1]), "d_ident": ident, "d_gfin": gfin})
        in_maps.append(m)
    res = run_bass_kernel_spmd(nc, in_maps, core_ids=list(range(NCORE)))
    yy = np.concatenate([res.results[c]["y"] for c in range(NCORE)], axis=0)
    return np.ascontiguousarray(yy[None].astype(np.float32))
```
